# Optimizing a Trainium2 kernel written in Bass

```python
import math
import jax, jax.numpy as jnp
from jax import lax
import numpy as np

D_MODEL = 1024
BATCH = 2
SEQ = 8192
DEPTH = 1

D_MIX = 2 * D_MODEL
D_SSD = D_MIX // 2
D_ATTN = D_MIX - D_SSD
SSD_HEAD_DIM = 64
SSD_HEADS = D_SSD // SSD_HEAD_DIM
SSD_GROUPS = 2
SSD_STATE = 128
CONV_WIDTH = 4
CHUNK = 128
D_XBC = D_SSD + 2 * SSD_GROUPS * SSD_STATE
ATTN_QK_DIM = 64
ATTN_HEADS = D_ATTN // (2 * ATTN_QK_DIM)
ATTN_V_DIM = 2 * ATTN_QK_DIM
D_QK = ATTN_HEADS * 2 * ATTN_QK_DIM
Q_BLOCK = 128
OFF_XBC = D_SSD
OFF_DT = OFF_XBC + D_XBC
OFF_Q = OFF_DT + SSD_HEADS
OFF_K = OFF_Q + D_QK
OFF_V = OFF_K + D_QK
OFF_ZA = OFF_V + ATTN_HEADS * ATTN_V_DIM
D_PROJ = OFF_ZA + D_ATTN
ALPHA = (2.0 * DEPTH) ** 0.25
BETA = (8.0 * DEPTH) ** -0.25
EPS = 1e-5

kernel_name = "hymba_ssd_diffattn_deepnorm_adaln"


def layer_norm(x, g, b):
    xf = x.astype(jnp.float32)
    mu = jnp.mean(xf, axis=-1, keepdims=True)
    var = jnp.mean(jnp.square(xf - mu), axis=-1, keepdims=True)
    y = (xf - mu) * lax.rsqrt(var + EPS)
    return (y * g + b).astype(x.dtype)


def rms_norm(x, w):
    xf = x.astype(jnp.float32)
    y = xf * lax.rsqrt(jnp.mean(jnp.square(xf), axis=-1, keepdims=True) + EPS)
    return (y * w).astype(x.dtype)


def group_rms_norm(y, w, groups):
    shp = y.shape
    yf = y.astype(jnp.float32).reshape(*shp[:-1], groups, shp[-1] // groups)
    yf = yf * lax.rsqrt(jnp.mean(jnp.square(yf), axis=-1, keepdims=True) + EPS)
    return yf.reshape(shp) * w


def causal_depthwise_conv(u, w, b):
    ch = u.shape[-1]
    out = lax.conv_general_dilated(
        u, w[:, None, :].astype(u.dtype), window_strides=(1,),
        padding=[(CONV_WIDTH - 1, 0)],
        dimension_numbers=('NWC', 'WIO', 'NWC'),
        feature_group_count=ch)
    return out + b


def decay_matrix(a_cum):
    t = a_cum.shape[-1]
    diff = a_cum[..., :, None] - a_cum[..., None, :]
    mask = jnp.tril(jnp.ones((t, t), dtype=bool))
    return jnp.exp(jnp.where(mask, diff, -jnp.inf))


def ssd_chunked(x, dt, a, bmat, cmat):
    bsz, seq, nh, hp = x.shape
    g, n = bmat.shape[2], bmat.shape[3]
    e = nh // g
    nc = seq // CHUNK
    xc = (x * dt[..., None]).reshape(bsz, nc, CHUNK, g, e, hp)
    a_dt = (dt * a).reshape(bsz, nc, CHUNK, g, e).transpose(0, 3, 4, 1, 2)
    bc = bmat.reshape(bsz, nc, CHUNK, g, n)
    cc = cmat.reshape(bsz, nc, CHUNK, g, n)
    a_cum = jnp.cumsum(a_dt, axis=-1)
    lmat = decay_matrix(a_cum)
    y_diag = jnp.einsum('bclgn,bcsgn,bgecls,bcsgep->bclgep', cc, bc, lmat, xc)
    decay_states = jnp.exp(a_cum[..., -1:] - a_cum)
    states = jnp.einsum('bclgn,bgecl,bclgep->bcgepn', bc, decay_states, xc)
    chunk_decay = jnp.exp(a_cum[..., -1])

    def step(hstate, inp):
        s_c, d_c = inp
        return hstate * d_c[..., None, None] + s_c, hstate

    init = jnp.zeros((bsz, g, e, hp, n), jnp.float32)
    _, prev = lax.scan(step, init, (states.transpose(1, 0, 2, 3, 4, 5),
                                    chunk_decay.transpose(3, 0, 1, 2)))
    y_off = jnp.einsum('bclgn,cbgepn,bgecl->bclgep', cc, prev, jnp.exp(a_cum))
    return (y_diag + y_off).reshape(bsz, seq, nh, hp)


def diff_attention(q, k, v, lam):
    bsz, seq, nh, _, dk = q.shape
    nb = seq // Q_BLOCK
    qb = (q * (dk ** -0.5)).reshape(bsz, nb, Q_BLOCK, nh, 2, dk).transpose(1, 0, 2, 3, 4, 5)
    k_pos = jnp.arange(seq)

    def block(args):
        q_blk, i = args
        s = jnp.einsum('bqhjd,bkhjd->bhjqk', q_blk, k).astype(jnp.float32)
        q_pos = i * Q_BLOCK + jnp.arange(Q_BLOCK)
        mask = k_pos[None, :] <= q_pos[:, None]
        p = jax.nn.softmax(jnp.where(mask, s, -jnp.inf), axis=-1)
        w = p[:, :, 0] - lam * p[:, :, 1]
        return jnp.einsum('bhqk,bkhd->bqhd', w.astype(v.dtype), v)

    out = lax.map(block, (qb, jnp.arange(nb)))
    return out.transpose(1, 0, 2, 3, 4).reshape(bsz, seq, nh, -1)


def hybrid_layer(x, c, w_ada, b_ada, w_in, conv_w, conv_b, dt_bias, a_log, d_skip,
                 ssd_norm_w, lambda_q1, lambda_k1, lambda_q2, lambda_k2, attn_norm_w,
                 w_out, ln_g, ln_b, layer_idx):
    bsz, seq, _ = x.shape
    mod = c @ w_ada + b_ada
    shift, scale, gate = jnp.split(mod, 3, axis=-1)
    h = x * (1.0 + scale[:, None, :]) + shift[:, None, :]
    proj = h @ w_in
    z_ssd, xbc, dt_raw, q, k, v, z_attn = jnp.split(
        proj, [OFF_XBC, OFF_DT, OFF_Q, OFF_K, OFF_V, OFF_ZA], axis=-1)

    xbc = jax.nn.silu(causal_depthwise_conv(xbc, conv_w, conv_b))
    xs, bm, cm = jnp.split(xbc, [D_SSD, D_SSD + SSD_GROUPS * SSD_STATE], axis=-1)
    dt = jax.nn.softplus(dt_raw.astype(jnp.float32) + dt_bias.astype(jnp.float32))
    a = -jnp.exp(a_log.astype(jnp.float32))
    xs_h = xs.astype(jnp.float32).reshape(bsz, seq, SSD_HEADS, SSD_HEAD_DIM)
    y = ssd_chunked(xs_h, dt, a,
                    bm.astype(jnp.float32).reshape(bsz, seq, SSD_GROUPS, SSD_STATE),
                    cm.astype(jnp.float32).reshape(bsz, seq, SSD_GROUPS, SSD_STATE))
    y = y + d_skip.astype(jnp.float32)[:, None] * xs_h
    y = y.reshape(bsz, seq, D_SSD) * jax.nn.silu(z_ssd.astype(jnp.float32))
    y_ssd = group_rms_norm(y, ssd_norm_w.astype(jnp.float32), SSD_GROUPS).astype(x.dtype)

    lambda_init = 0.8 - 0.6 * math.exp(-0.3 * layer_idx)
    lam = (jnp.exp(jnp.sum(lambda_q1.astype(jnp.float32) * lambda_k1.astype(jnp.float32)))
           - jnp.exp(jnp.sum(lambda_q2.astype(jnp.float32) * lambda_k2.astype(jnp.float32)))
           + lambda_init)
    o = diff_attention(q.reshape(bsz, seq, ATTN_HEADS, 2, ATTN_QK_DIM),
                       k.reshape(bsz, seq, ATTN_HEADS, 2, ATTN_QK_DIM),
                       v.reshape(bsz, seq, ATTN_HEADS, ATTN_V_DIM), lam)
    o = rms_norm(o, attn_norm_w) * (1.0 - lambda_init)
    y_attn = o.reshape(bsz, seq, D_ATTN) * jax.nn.silu(z_attn)

    mixed = jnp.concatenate([y_ssd, y_attn], axis=-1) @ w_out
    return layer_norm(ALPHA * x + gate[:, None, :] * mixed, ln_g, ln_b)


def setup_inputs(seed: int = 0) -> dict:
    key = jax.random.key(seed)
    ks = jax.random.split(key, 20)
    f32 = jnp.float32
    x = jax.random.normal(ks[0], (BATCH, SEQ, D_MODEL), f32)
    c = jax.random.normal(ks[1], (BATCH, D_MODEL), f32)
    w_ada = jax.random.normal(ks[2], (DEPTH, D_MODEL, 3 * D_MODEL), f32) * (0.5 * D_MODEL ** -0.5)
    b_ada = jax.random.normal(ks[3], (DEPTH, 3 * D_MODEL), f32) * 0.01
    w_in = jax.random.normal(ks[4], (DEPTH, D_MODEL, D_PROJ), f32) * D_MODEL ** -0.5
    conv_w = jax.random.normal(ks[5], (DEPTH, CONV_WIDTH, D_XBC), f32) * CONV_WIDTH ** -0.5
    conv_b = jax.random.normal(ks[6], (DEPTH, D_XBC), f32) * 0.01
    dt0 = jnp.exp(jax.random.uniform(ks[7], (DEPTH, SSD_HEADS), f32,
                                     math.log(1e-3), math.log(1e-1)))
    dt_bias = dt0 + jnp.log(-jnp.expm1(-dt0))
    a_log = jnp.log(jax.random.uniform(ks[8], (DEPTH, SSD_HEADS), f32, 1.0, 16.0))
    d_skip = 1.0 + 0.1 * jax.random.normal(ks[9], (DEPTH, SSD_HEADS), f32)
    ssd_norm_w = 1.0 + 0.01 * jax.random.normal(ks[10], (DEPTH, D_SSD), f32)
    lambda_q1 = 0.1 * jax.random.normal(ks[11], (DEPTH, ATTN_QK_DIM), f32)
    lambda_k1 = 0.1 * jax.random.normal(ks[12], (DEPTH, ATTN_QK_DIM), f32)
    lambda_q2 = 0.1 * jax.random.normal(ks[13], (DEPTH, ATTN_QK_DIM), f32)
    lambda_k2 = 0.1 * jax.random.normal(ks[14], (DEPTH, ATTN_QK_DIM), f32)
    attn_norm_w = 1.0 + 0.01 * jax.random.normal(ks[15], (DEPTH, ATTN_V_DIM), f32)
    w_out = jax.random.normal(ks[16], (DEPTH, D_MIX, D_MODEL), f32) * (D_MIX ** -0.5 * BETA)
    ln_g = 1.0 + 0.01 * jax.random.normal(ks[17], (DEPTH, D_MODEL), f32)
    ln_b = 0.01 * jax.random.normal(ks[18], (DEPTH, D_MODEL), f32)
    return {"x": x, "c": c, "w_ada": w_ada, "b_ada": b_ada, "w_in": w_in,
            "conv_w": conv_w, "conv_b": conv_b, "dt_bias": dt_bias, "a_log": a_log,
            "d_skip": d_skip, "ssd_norm_w": ssd_norm_w, "lambda_q1": lambda_q1,
            "lambda_k1": lambda_k1, "lambda_q2": lambda_q2, "lambda_k2": lambda_k2,
            "attn_norm_w": attn_norm_w, "w_out": w_out, "ln_g": ln_g, "ln_b": ln_b}


def reference(x, c, w_ada, b_ada, w_in, conv_w, conv_b, dt_bias, a_log, d_skip,
              ssd_norm_w, lambda_q1, lambda_k1, lambda_q2, lambda_k2, attn_norm_w,
              w_out, ln_g, ln_b):
    for l in range(DEPTH):
        x = hybrid_layer(x, c, w_ada[l], b_ada[l], w_in[l], conv_w[l], conv_b[l],
                         dt_bias[l], a_log[l], d_skip[l], ssd_norm_w[l],
                         lambda_q1[l], lambda_k1[l], lambda_q2[l], lambda_k2[l],
                         attn_norm_w[l], w_out[l], ln_g[l], ln_b[l], l)
    return x
```

```python
import math
from contextlib import ExitStack

import numpy as np
import concourse.bass as bass
import concourse.mybir as mybir
from concourse.bass_utils import run_bass_kernel_spmd

F32 = mybir.dt.float32
BF16 = mybir.dt.bfloat16
AF = mybir.ActivationFunctionType
ALU = mybir.AluOpType
AX = mybir.AxisListType

D_MODEL = 1024
SEQ = 8192
D_SSD = 1024
D_XBC = 1536
OFF_XBC = 1024
OFF_DT = OFF_XBC + D_XBC
OFF_Q = OFF_DT + 16
OFF_K = OFF_Q + 1024
OFF_V = OFF_K + 1024
OFF_ZA = OFF_V + 1024
D_PROJ = OFF_ZA + 1024
ALPHA = 2.0 ** 0.25
EPS = 1e-5
LAMBDA_INIT = 0.8 - 0.6 * math.exp(0.0)
NCOLS = 1800
C_V = 1536
NEG = -30000.0

ENGS = ["pe", "act", "dve", "pool", "sp"]


class Prog:
    def __init__(self):
        self.streams = {e: [] for e in ENGS}
        self.nops = {e: 0 for e in ENGS}
        self.seen = {e: {} for e in ENGS}
        self.state = {}
        self.dma_val = {}
        self.milestones = {e: set() for e in ENGS}

    def _need(self, needed, tok):
        k = tok[0]
        if k not in needed or needed[k][1] < tok[1]:
            needed[k] = tok

    def _emit_waits(self, eng, needed):
        for k, tok in needed.items():
            if self.seen[eng].get(k, 0) >= tok[1]:
                continue
            self.seen[eng][k] = tok[1]
            self.streams[eng].append(("wait", tok))
            if tok[2] == "eng":
                self.milestones[k].add(tok[1])

    def op(self, eng, fn, reads=(), writes=(), dma_sem=None, inc=16):
        needed = {}
        for k in reads:
            st = self.state.get(k)
            if st and st["w"]:
                self._need(needed, st["w"])
        for k in writes:
            st = self.state.get(k)
            if st:
                if st["w"]:
                    t = st["w"]
                    if not (t[2] == "eng" and t[0] == eng) and not (t[2] == "dma" and t[0] == dma_sem):
                        self._need(needed, t)
                for t in st["r"].values():
                    if not (t[2] == "eng" and t[0] == eng):
                        self._need(needed, t)
        if eng == "pe":
            needed.pop("pe", None)
        self._emit_waits(eng, needed)
        if dma_sem is None:
            self.nops[eng] += 1
            tok = (eng, self.nops[eng], "eng")
            self.streams[eng].append(("op", fn, self.nops[eng]))
        else:
            self.dma_val[dma_sem] = self.dma_val.get(dma_sem, 0) + inc
            tok = (dma_sem, self.dma_val[dma_sem], "dma")
            self.streams[eng].append(("dma", fn, dma_sem, inc))
        for k in reads:
            st = self.state.setdefault(k, {"w": None, "r": {}})
            st["r"][tok[0]] = tok
        for k in writes:
            self.state[k] = {"w": tok, "r": {}}
        return tok

    def wait_all(self, eng):
        needed = {}
        for e in ENGS:
            if e != eng and self.nops[e] > 0:
                self._need(needed, (e, self.nops[e], "eng"))
        for s, v in self.dma_val.items():
            self._need(needed, (s, v, "dma"))
        self._emit_waits(eng, needed)

    def barrier(self):
        snap_ops = dict(self.nops)
        snap_dma = dict(self.dma_val)
        for eng in ENGS:
            needed = {}
            for e in ENGS:
                if e != eng and snap_ops[e] > 0:
                    self._need(needed, (e, snap_ops[e], "eng"))
            for s, v in snap_dma.items():
                if s.startswith("cc"):
                    continue
                self._need(needed, (s, v, "dma"))
            self._emit_waits(eng, needed)

    def replay(self, eng, engine_obj, sems, dma_sems):
        ms = sorted(self.milestones[eng])
        rank = {v: i + 1 for i, v in enumerate(ms)}
        for item in self.streams[eng]:
            if item[0] == "wait":
                tok = item[1]
                if tok[2] == "eng":
                    msl = sorted(self.milestones[tok[0]])
                    import bisect
                    val = bisect.bisect_right(msl, tok[1])
                    engine_obj.wait_ge(sems[tok[0]], val)
                else:
                    engine_obj.wait_ge(dma_sems[tok[0]], tok[1])
            elif item[0] == "op":
                inst = item[1](engine_obj)
                if item[2] in rank:
                    inst.then_inc(sems[eng], 1)
            else:
                inst = item[1](engine_obj)
                if item[3] == 16:
                    inst.then_inc(dma_sems[item[2]], 16)
                else:
                    inst.then_inc(dma_sems[item[2]])


def build_program(L, debug=False, stop=9):
    NT = L // 512
    LQ = L // 4
    NAG = max(L // 1024, 1)
    AGW = L // NAG
    NS3 = NAG
    ST3 = AGW // 4
    nc = bass.Bass("TRN2", target_bir_lowering=False)
    P = Prog()

    def din(name, shape, dt=F32):
        return nc.dram_tensor(name, list(shape), dt, kind="ExternalInput").ap()

    xT_d = din("xT", [8, 128, L])
    xtok_d = din("xtok", [LQ, 1024])
    ccol_d = din("ccol", [128, 8])
    wada_d = din("wada", [8, 128, 3072])
    bada_col_d = din("bada_col", [128, 24])
    bgate_d = din("bgate", [128, 1024])
    win_d = din("win", [8, 128, NCOLS])
    convw_d = din("convw", [128, 16])
    convb_d = din("convb", [128, 4])
    dtb_d = din("dtb", [128, 16])
    alog_d = din("alog", [128, 16])
    dskip_d = din("dskip", [128, 2])
    lamv_d = din("lamv", [128, 256])
    anw_d = din("anw", [128, 1])
    nw_d = din("nw", [128, 16])
    wout_d = din("wout", [16, 128, 1024])
    lng_d = din("lng", [128, 1024])
    lnb_d = din("lnb", [128, 1024])
    cst_d = din("cst", [128, 5 * 128])
    out_d = nc.dram_tensor("out", [LQ, 1024], F32, kind="ExternalOutput").ap()
    ybufs = [nc.dram_tensor("ybuf%d" % a, [512, AGW], BF16) for a in range(NAG)]
    ygaths = [nc.dram_tensor("ygath%d" % a, [2048, AGW], BF16) for a in range(NAG)]
    if debug:
        dbg_d = nc.dram_tensor("dbg", [2048, L], BF16, kind="ExternalOutput").ap()

    es = ExitStack()
    with es:
        def sb(name, shape, dt=F32):
            return es.enter_context(nc.sbuf_tensor(name, list(shape), dt))

        sems = {e: es.enter_context(nc.semaphore("s_" + e)) for e in ENGS}
        dma_names = ["xT0", "xT1", "cst", "win", "youtS", "youtA", "wout", "yt0", "yt1", "xt0", "xt1",
                     "o0", "o1", "misc", "dbg"]
        dma_names += ["cc%d" % a for a in range(NAG)]
        dma_sems = {n: es.enter_context(nc.semaphore("d_" + n)) for n in dma_names}

        xTs = [sb("xTs%d" % i, [128, 8, 512]) for i in range(2)]
        hT = sb("hT", [128, 8, 512], BF16)
        win_sb = sb("win_sb", [128, 8, NCOLS], BF16)
        KT = sb("KT", [128, 2, L], BF16)
        Vt = sb("Vt", [128, L // 128, 256], BF16)
        QTs = [sb("QT%d" % i, [128, 2, 2, 512], BF16) for i in range(2)]
        zaTs = [sb("zaT%d" % i, [128, 2, 512]) for i in range(2)]
        zsT = sb("zsT", [128, 2, 512])
        ubuf = sb("ubuf", [128, 4, 515])
        cacc = sb("cacc", [128, 2, 512])
        xbc = sb("xbc", [128, 4, 512], BF16)
        etmp = sb("etmp", [128, 512])
        cbc = cacc[:, 0:2, :].rearrange("p a (b c) -> p (a b) c", c=128)
        dtraw = sb("dtraw", [128, 16])
        dtv = sb("dtv", [128, 16])
        adt = sb("adt", [128, 16])
        arep = sb("arep", [128, 16])
        dtb = sb("dtb_sb", [128, 16])
        atok = sb("atok", [128, 8])
        dtw = sb("dtw", [128, 4])
        adtbc = sb("adtbc", [128, 4, 128])
        Dm = sb("Dm", [128, 4, 128])
        LT = sb("LT", [128, 4, 128])
        Erow = sb("Erow", [128, 4, 128])
        MT = sb("MT", [128, 4, 128], BF16)
        CeT = sb("CeT", [128, 4, 128], BF16)
        xbtok = sb("xbtok", [128, 384], BF16)
        xdt = sb("xdt", [128, 256], BF16)
        xdtw = sb("xdtw", [128, 256], BF16)
        Hs = sb("Hs", [128, 256])
        Hbf = sb("Hbf", [128, 256], BF16)
        ytmp = sb("ytmp", [128, 128])
        yS = sb("yS", [128, 2, 512], BF16)
        yA = sb("yA", [128, 2, 512], BF16)
        PT = [sb("PT%d" % i, [128, 2, 2, 256], BF16) for i in range(2)]
        rl = sb("rl", [128, 2, 256])
        On = sb("On", [128, 2, 256])
        Od = sb("Od", [128, 256])
        sqb = sb("sqb", [128, 256], BF16)
        rstd = sb("rstd", [128, 256])
        lamv = On[:].rearrange("p j q -> p (j q)")[:, 0:256]
        lamt = Od[:, 0:128]
        cst = sb("cst_sb", [128, 640])
        cstb = sb("cstb", [128, 640], BF16)
        ccol = sb("ccol_sb", [128, 8])
        modT = sb("modT", [128, 24])
        badac = sb("badac", [128, 24])
        gate_bc = sb("gate_bc", [128, 1024])
        convw = sb("convw_sb", [128, 16])
        convb = sb("convb_sb", [128, 4])
        dskip = sb("dskip_sb", [128, 2])
        lams = sb("lams", [128, 4])
        neglam = sb("neglam", [128, 1])
        anw = sb("anw_sb", [128, 1])
        nw = sb("nw_sb", [128, 16])
        small = sb("small", [128, 16])

        tri_incl = cst[:, 0:128]
        tri_su = cst[:, 128:256]
        negmT = cst[:, 256:384]
        ones_f = cst[:, 384:512]
        ident_b = cstb[:, 512:640]
        ones_b = cstb[:, 384:512]
        triT_b = cstb[:, 0:128]
        negm_b = cstb[:, 256:384]

        Sbig = es.enter_context(nc.psum_tensor("Sbig", [128, 2048], F32))
        banks = [None] * 4 + [es.enter_context(nc.psum_tensor("bank%d" % i, [128, 512], F32)) for i in range(4, 8)]
        gen_state = {"i": 0}

        def gen_bank():
            i = 6 + (gen_state["i"] % 2)
            gen_state["i"] += 1
            return banks[i], "ps%d" % i

        P.op("sp", lambda e: e.dma_start(out=cst[:], in_=cst_d), writes=["cst"], dma_sem="cst")
        misc_keys = []
        for (t_sb, t_d, key) in [(ccol, ccol_d, "ccol"), (badac, bada_col_d, "badac"),
                                 (convw, convw_d, "convw"), (convb, convb_d, "convb"),
                                 (dtb, dtb_d, "dtb"), (arep, alog_d, "arep"),
                                 (dskip, dskip_d, "dskip"), (lamv, lamv_d, "lamv"),
                                 (anw, anw_d, "anw"), (nw, nw_d, "nw"),
                                 (gate_bc, bgate_d, "gate_bc")]:
            misc_tok = P.op("sp", (lambda e, a=t_sb, b=t_d, key=key: e.dma_start(out=(a if key == "lamv" else a[:]), in_=b)),
                            writes=[key], dma_sem="misc")
            misc_keys.append(key)
        for key in misc_keys:
            P.state[key]["w"] = misc_tok
        for kc in range(8):
            P.op("pool", (lambda e, kc=kc: e.dma_start(out=win_sb[:, kc, :], in_=win_d[kc])),
                 writes=["win"], dma_sem="win")
        P.op("dve", lambda e: e.tensor_copy(out=cstb[:], in_=cst[:]), reads=["cst"], writes=["cstb"])
        P.op("act", lambda e: e.activation(out=arep[:], in_=arep[:], func=AF.Exp),
             reads=["arep"], writes=["arep"])
        P.op("dve", lambda e: e.tensor_scalar(out=arep[:], in0=arep[:], scalar1=-1.0, scalar2=None,
                                              op0=ALU.mult), reads=["arep"], writes=["arep"])
        P.op("dve", lambda e: e.tensor_tensor(out=lamt[:, 0:64], in0=lamv[:, 0:64], in1=lamv[:, 64:128],
                                              op=ALU.mult), reads=["lamv"], writes=["lamt0"])
        P.op("dve", lambda e: e.tensor_tensor(out=lamt[:, 64:128], in0=lamv[:, 128:192],
                                              in1=lamv[:, 192:256], op=ALU.mult),
             reads=["lamv"], writes=["lamt1"])
        P.op("dve", lambda e: e.tensor_reduce(out=lams[:, 0:1], in_=lamt[:, 0:64], axis=AX.X, op=ALU.add),
             reads=["lamt0"], writes=["lams0"])
        P.op("dve", lambda e: e.tensor_reduce(out=lams[:, 1:2], in_=lamt[:, 64:128], axis=AX.X, op=ALU.add),
             reads=["lamt1"], writes=["lams1"])
        P.op("act", lambda e: e.activation(out=lams[:, 2:4], in_=lams[:, 0:2], func=AF.Exp),
             reads=["lams0", "lams1"], writes=["lams2"])
        P.op("dve", lambda e: e.tensor_tensor(out=neglam[:], in0=lams[:, 3:4], in1=lams[:, 2:3],
                                              op=ALU.subtract), reads=["lams2"], writes=["neglam"])
        P.op("dve", lambda e: e.tensor_scalar(out=neglam[:], in0=neglam[:], scalar1=-LAMBDA_INIT,
                                              scalar2=None, op0=ALU.add), reads=["neglam"], writes=["neglam"])
        P.op("dve", lambda e: e.tensor_scalar(out=anw[:], in0=anw[:], scalar1=1.0 - LAMBDA_INIT,
                                              scalar2=None, op0=ALU.mult), reads=["anw"], writes=["anw"])
        for kc in range(8):
            P.op("dve", (lambda e, kc=kc: e.tensor_scalar(out=cbc[:, kc, :], in0=ones_f,
                                                          scalar1=ccol[:, kc:kc + 1], scalar2=None,
                                                          op0=ALU.mult)),
                 reads=["cst", "ccol"], writes=["cbc"])
        for cb in range(6):
            slot = cb % 2
            for kc in range(8):
                P.op("sp", (lambda e, cb=cb, slot=slot, kc=kc: e.dma_start(
                    out=xTs[slot][:, kc, :], in_=wada_d[kc, :, cb * 512:(cb + 1) * 512])),
                     writes=["xTs%d" % slot], dma_sem="xT%d" % slot)
            if cb < 4:
                bank, bkey = gen_bank()
                for j4 in range(4):
                    j = cb * 4 + j4
                    for kc in range(8):
                        P.op("pe", (lambda e, bank=bank, j4=j4, kc=kc, slot=slot: e.matmul(
                            out=bank[:, j4:j4 + 1], lhsT=xTs[slot][:, kc, j4 * 128:(j4 + 1) * 128],
                            rhs=ccol[:, kc:kc + 1], start=(kc == 0), stop=(kc == 7))),
                             reads=["xTs%d" % slot, "ccol"], writes=[bkey])
                P.op("dve", (lambda e, bank=bank, cb=cb: e.tensor_tensor(
                    out=modT[:, cb * 4:cb * 4 + 4], in0=bank[:, 0:4], in1=badac[:, cb * 4:cb * 4 + 4],
                    op=ALU.add)), reads=[bkey, "badac"], writes=["modT"])
            else:
                bank, bkey = gen_bank()
                for kc in range(8):
                    P.op("pe", (lambda e, bank=bank, kc=kc, slot=slot: e.matmul(
                        out=bank[:, :], lhsT=cbc[:, kc, :], rhs=xTs[slot][:, kc, :],
                        start=(kc == 0), stop=(kc == 7))),
                         reads=["xTs%d" % slot, "cbc"], writes=[bkey])
                c0 = (cb - 4) * 512
                P.op("dve", (lambda e, bank=bank, c0=c0: e.tensor_tensor(
                    out=gate_bc[:, c0:c0 + 512], in0=bank[:, :], in1=gate_bc[:, c0:c0 + 512], op=ALU.add)),
                     reads=[bkey, "gate_bc"], writes=["gate_bc"])
        P.op("dve", lambda e: e.tensor_scalar(out=modT[:, 8:16], in0=modT[:, 8:16], scalar1=1.0,
                                              scalar2=None, op0=ALU.add), reads=["modT"], writes=["modT"])
        P.op("pool", lambda e: e.memset(ubuf[:], 0.0), writes=["ubuf"])
        P.op("pool", lambda e: e.memset(Hs[:], 0.0), writes=["Hs"])
        P.op("pool", lambda e: e.memset(Hbf[:], 0.0), writes=["Hbf"])
        for qi_ in range(2):
            P.op("pool", (lambda e, qi_=qi_: e.memset(QTs[qi_][:], 0.0)), writes=["QT%d_0" % qi_, "QT%d_1" % qi_])

        def silu_to(src_ap, src_keys, dst_ap, dst_keys, shape_cols, dve_eng="dve"):
            P.op("act", lambda e: e.activation(out=etmp[:, 0:shape_cols], in_=src_ap, func=AF.Exp, scale=-1.0),
                 reads=src_keys, writes=["etmp"])
            P.op("dve", lambda e: e.tensor_scalar(out=etmp[:, 0:shape_cols], in0=etmp[:, 0:shape_cols],
                                                  scalar1=1.0, scalar2=None, op0=ALU.add),
                 reads=["etmp"], writes=["etmp"])
            P.op("dve", lambda e: e.reciprocal(out=etmp[:, 0:shape_cols], in_=etmp[:, 0:shape_cols]),
                 reads=["etmp"], writes=["etmp"])
            P.op("dve", lambda e: e.tensor_tensor(out=dst_ap, in0=src_ap, in1=etmp[:, 0:shape_cols],
                                                  op=ALU.mult),
                 reads=src_keys + ["etmp"], writes=dst_keys)

        def load_h(T):
            t0 = T * 512
            slot = T % 2
            xs_key = "xTs%d" % slot
            for kc in range(8):
                P.op("sp", (lambda e, kc=kc, slot=slot, t0=t0: e.dma_start(
                    out=xTs[slot][:, kc, :], in_=xT_d[kc, :, t0:t0 + 512])),
                     writes=[xs_key], dma_sem="xT%d" % slot)
            for kc in range(8):
                P.op("dve", (lambda e, kc=kc, slot=slot: e.tensor_scalar(
                    out=hT[:, kc, :], in0=xTs[slot][:, kc, :], scalar1=modT[:, 8 + kc:9 + kc],
                    scalar2=modT[:, kc:kc + 1], op0=ALU.mult, op1=ALU.add)),
                     reads=[xs_key, "modT"], writes=["hT%d" % kc])

        def other_gen(T):
            t0 = T * 512
            slot = T % 2
            xs_key = "xTs%d" % slot
            QT = QTs[T % 2]
            zaT = zaTs[T % 2]
            par = T % 2
            hkeys = ["hT%d" % kc for kc in range(8)]
            wkeys = ["win"] * 8

            def inproj(g):
                bank, bkey = gen_bank()
                for kc in range(8):
                    P.op("pe", (lambda e, bank=bank, kc=kc, g=g: e.matmul(
                        out=bank[:, :], lhsT=win_sb[:, kc, g * 128:(g + 1) * 128], rhs=hT[:, kc, :],
                        start=(kc == 0), stop=(kc == 7))),
                         reads=[hkeys[kc], wkeys[kc]], writes=[bkey])
                return bank, bkey

            for hd in range(2):
                bank, bkey = inproj(8 + hd)
                P.op("dve", (lambda e, bank=bank, hd=hd, t0=t0: e.tensor_copy(
                    out=KT[:, hd, t0:t0 + 512], in_=bank[:, :])),
                     reads=[bkey], writes=["KT%d_%d" % (hd, T)])
                yield
            for hd in range(2):
                bank, bkey = inproj(6 + hd)
                for j in range(2):
                    P.op("dve", (lambda e, bank=bank, hd=hd, j=j: e.tensor_copy(
                        out=QT[64 * j:64 * j + 64, hd, j, :], in_=bank[64 * j:64 * j + 64, :])),
                         reads=[bkey], writes=["QT%d_%d" % (par, hd)])
                yield
            for s in range(4):
                bank, bkey = gen_bank()
                for kc in range(8):
                    P.op("pe", (lambda e, bank=bank, kc=kc, s=s: e.matmul(
                        out=bank[:, 0:260], lhsT=hT[:, kc, s * 128:(s + 1) * 128],
                        rhs=win_sb[:, kc, C_V:C_V + 260], start=(kc == 0), stop=(kc == 7))),
                         reads=[hkeys[kc], wkeys[kc]], writes=[bkey])
                kt = T * 4 + s
                P.op("dve", (lambda e, bank=bank, kt=kt: e.tensor_copy(out=Vt[:, kt, :], in_=bank[:, 0:256])),
                     reads=[bkey], writes=["V%d" % kt])
                P.op("dve", (lambda e, bank=bank, s=s: e.tensor_copy(out=dtraw[:, 4 * s:4 * s + 4],
                                                                     in_=bank[:, 256:260])),
                     reads=[bkey], writes=["dtraw"])
                yield
            if stop >= 1.6:
                P.op("dve", lambda e: e.tensor_tensor(out=dtv[:], in0=dtraw[:], in1=dtb[:], op=ALU.add),
                     reads=["dtraw", "dtb"], writes=["dtv"])
                P.op("act", lambda e: e.activation(out=dtv[:], in_=dtv[:], func=AF.Exp), reads=["dtv"], writes=["dtv"])
                P.op("act", lambda e: e.activation(out=dtv[:], in_=dtv[:], func=AF.Ln, bias=1.0),
                     reads=["dtv"], writes=["dtv"])
                P.op("dve", lambda e: e.tensor_tensor(out=adt[:], in0=dtv[:], in1=arep[:], op=ALU.mult),
                     reads=["dtv", "arep"], writes=["adt"])

            yield
            for ch in range(4 if stop >= 1.4 else 0):
                bank, bkey = inproj(2 + ch)
                P.op("act", (lambda e, bank=bank, ch=ch: e.activation(out=ubuf[:, ch, 3:515], in_=bank[:, :],
                                                                      func=AF.Copy)),
                     reads=[bkey], writes=["ubuf%d" % ch])
                P.op("dve", (lambda e, ch=ch: e.tensor_scalar(
                    out=cacc[:, ch % 2, :], in0=ubuf[:, ch, 0:512], scalar1=convw[:, 4 * ch:4 * ch + 1],
                    scalar2=convb[:, ch:ch + 1], op0=ALU.mult, op1=ALU.add)),
                     reads=["ubuf%d" % ch, "ubuf", "convw", "convb"], writes=["cacc%d" % (ch % 2), "cbc"])
                for k in range(1, 4):
                    P.op("dve", (lambda e, ch=ch, k=k: e.scalar_tensor_tensor(
                        out=cacc[:, ch % 2, :], in0=ubuf[:, ch, k:k + 512],
                        scalar=convw[:, 4 * ch + k:4 * ch + k + 1], in1=cacc[:, ch % 2, :],
                        op0=ALU.mult, op1=ALU.add)),
                         reads=["ubuf%d" % ch, "cacc%d" % (ch % 2)], writes=["cacc%d" % (ch % 2)])
                P.op("pool", (lambda e, ch=ch: e.tensor_copy(out=ubuf[:, ch, 0:3], in_=ubuf[:, ch, 512:515])),
                     reads=["ubuf%d" % ch], writes=["ubuf%d" % ch])
                silu_to(cacc[:, ch % 2, :], ["cacc%d" % (ch % 2)], xbc[:, ch, :], ["xbc%d" % ch], 512)
                yield

            for hd in range(2 if stop >= 1.4 else 0):
                bank, bkey = inproj(10 + hd)
                silu_to(bank[:, :], [bkey], zaT[:, hd, :], ["zaT%d_%d" % (par, hd)], 512)
                yield
            for i in range(2 if stop >= 1.4 else 0):
                bank, bkey = inproj(i)
                silu_to(bank[:, :], [bkey], zsT[:, i, :], ["zsT%d" % i], 512)
                yield
            for s in range(4 if stop >= 1.6 else 0):
                o = s * 128
                bk7, key7 = gen_bank()
                P.op("pe", (lambda e, s=s, bk7=bk7: e.matmul(out=bk7[:, 0:4], lhsT=tri_incl, rhs=adt[:, 4 * s:4 * s + 4],
                                                    start=True, stop=True)),
                     reads=["cst", "adt"], writes=[key7])
                P.op("pe", (lambda e, s=s, bk7=bk7: e.matmul(out=bk7[:, 4:8], lhsT=tri_su, rhs=adt[:, 4 * s:4 * s + 4],
                                                    start=True, stop=True)),
                     reads=["cst", "adt"], writes=[key7])
                P.op("pe", (lambda e, o=o, bk7=bk7: e.matmul(out=bk7[:, 128:256], lhsT=xbc[:, 2, o:o + 128],
                                                             rhs=xbc[:, 3, o:o + 128], start=True, stop=True)),
                     reads=["xbc2", "xbc3"], writes=[key7])
                P.op("dve", (lambda e, bk7=bk7: e.tensor_copy(out=atok[:], in_=bk7[:, 0:8])),
                     reads=[key7], writes=["atok"])
                P.op("act", lambda e: e.activation(out=dtw[:], in_=atok[:, 4:8], func=AF.Exp),
                     reads=["atok"], writes=["dtw"])
                P.op("dve", (lambda e, s=s: e.tensor_tensor(out=dtw[:], in0=dtw[:], in1=dtv[:, 4 * s:4 * s + 4],
                                                            op=ALU.mult)),
                     reads=["dtw", "dtv"], writes=["dtw"])
                yield
                for h in range(4):
                    P.op("dve", (lambda e, h=h, s=s: e.tensor_scalar(
                        out=adtbc[:, h, :], in0=ones_f, scalar1=adt[:, 4 * s + h:4 * s + h + 1], scalar2=None,
                        op0=ALU.mult)), reads=["cst", "adt"], writes=["adtbc%d" % h])
                yield
                yield
                yield
                bk6, key6 = gen_bank()
                for h in range(4):
                    P.op("pe", (lambda e, h=h, bk6=bk6: e.matmul(out=bk6[:, h * 128:(h + 1) * 128],
                                                        lhsT=adtbc[:, h, :], rhs=tri_incl, start=True, stop=True)),
                         reads=["adtbc%d" % h, "cst"], writes=[key6])
                yield
                for h in range(4):
                    P.op("dve", (lambda e, h=h, bk6=bk6: e.scalar_tensor_tensor(
                        out=Dm[:, h, :], in0=bk6[:, h * 128:(h + 1) * 128], scalar=atok[:, h:h + 1],
                        in1=negmT, op0=ALU.subtract, op1=ALU.add)),
                         reads=[key6, "atok", "cst"], writes=["Dm"])
                P.op("act", lambda e: e.activation(out=LT[:].rearrange("p h l -> p (h l)"),
                                                   in_=Dm[:].rearrange("p h l -> p (h l)"), func=AF.Exp),
                     reads=["Dm"], writes=["LT"])
                P.op("act", (lambda e, bk6=bk6: e.activation(out=Erow[:].rearrange("p h l -> p (h l)"),
                                                             in_=bk6[:, :], func=AF.Exp)),
                     reads=[key6], writes=["Erow"])
                yield
                for h in range(4):
                    P.op("dve", (lambda e, h=h, bk7=bk7: e.tensor_tensor(out=MT[:, h, :], in0=LT[:, h, :],
                                                                in1=bk7[:, 128:256], op=ALU.mult)),
                         reads=["LT", key7], writes=["MT"])
                    P.op("dve", (lambda e, h=h, o=o: e.tensor_tensor(out=CeT[:, h, :], in0=Erow[:, h, :],
                                                                      in1=xbc[:, 3, o:o + 128], op=ALU.mult)),
                         reads=["Erow", "xbc3"], writes=["CeT"])
                yield
                bank, bkey = gen_bank()
                bview = bank.bitcast(BF16)
                for i in range(3):
                    P.op("pe", (lambda e, bview=bview, i=i, o=o: e.transpose(
                        out=bview[:, i * 128:(i + 1) * 128], in_=xbc[:, i, o:o + 128], identity=ident_b)),
                         reads=["xbc%d" % i, "cstb"], writes=[bkey])
                P.op("dve", (lambda e, bview=bview: e.tensor_copy(out=xbtok[:], in_=bview[:, 0:384])),
                     reads=[bkey], writes=["xbtok"])
                for h in range(4):
                    P.op("dve", (lambda e, h=h, s=s: e.tensor_scalar(
                        out=xdt[:, 64 * h:64 * h + 64], in0=xbtok[:, 64 * h:64 * h + 64],
                        scalar1=dtv[:, 4 * s + h:4 * s + h + 1], scalar2=None, op0=ALU.mult)),
                         reads=["xbtok", "dtv"], writes=["xdt"])
                    P.op("dve", (lambda e, h=h: e.tensor_scalar(
                        out=xdtw[:, 64 * h:64 * h + 64], in0=xbtok[:, 64 * h:64 * h + 64],
                        scalar1=dtw[:, h:h + 1], scalar2=None, op0=ALU.mult)),
                         reads=["xbtok", "dtw"], writes=["xdtw"])
                yield
                yield
                yield
                for i in range(2):
                    bank, bkey = gen_bank()
                    for hh in range(2):
                        h = 2 * i + hh
                        P.op("pe", (lambda e, bank=bank, hh=hh, h=h: e.matmul(
                            out=bank[64 * hh:64 * hh + 64, 0:128], lhsT=xdt[:, 64 * h:64 * h + 64],
                            rhs=MT[:, h, :], start=True, stop=False, tile_position=(0, 64 * hh))),
                             reads=["xdt", "MT"], writes=[bkey])
                        P.op("pe", (lambda e, bank=bank, hh=hh, h=h: e.matmul(
                            out=bank[64 * hh:64 * hh + 64, 0:128], lhsT=Hbf[:, 64 * h:64 * h + 64],
                            rhs=CeT[:, h, :], start=False, stop=True, tile_position=(0, 64 * hh))),
                             reads=["Hbf", "CeT"], writes=[bkey])
                    P.op("dve", (lambda e, bank=bank, i=i, o=o: e.scalar_tensor_tensor(
                        out=ytmp[:], in0=xbc[:, i, o:o + 128], scalar=dskip[:, i:i + 1], in1=bank[:, 0:128],
                        op0=ALU.mult, op1=ALU.add)),
                         reads=[bkey, "xbc%d" % i, "dskip"], writes=["ytmp"])
                    P.op("pool", (lambda e, i=i, o=o: e.tensor_tensor(
                        out=yS[:, i, o:o + 128], in0=ytmp[:], in1=zsT[:, i, o:o + 128], op=ALU.mult)),
                         reads=["ytmp", "zsT%d" % i], writes=["yS"])
                yield
                bank, bkey = gen_bank()
                P.op("pe", (lambda e, bank=bank: e.matmul(out=bank[:, 0:256], lhsT=xbtok[:, 256:384],
                                                          rhs=xdtw[:], start=True, stop=True)),
                     reads=["xbtok", "xdtw"], writes=[bkey])
                for h in range(4):
                    P.op("dve", (lambda e, bank=bank, h=h: e.scalar_tensor_tensor(
                        out=Hs[:, 64 * h:64 * h + 64], in0=Hs[:, 64 * h:64 * h + 64],
                        scalar=Erow[:, h, 127:128], in1=bank[:, 64 * h:64 * h + 64],
                        op0=ALU.mult, op1=ALU.add)),
                         reads=[bkey, "Erow", "Hs"], writes=["Hs"])
                P.op("dve", lambda e: e.tensor_copy(out=Hbf[:], in_=Hs[:]), reads=["Hs"], writes=["Hbf"])
                yield
            for i in range(2 if stop >= 1.6 else 0):
                P.op("sp", (lambda e, i=i, t0=t0: e.dma_start(
                    out=ybufs[t0 // AGW].ap()[128 * i:128 * i + 128, t0 % AGW:t0 % AGW + 512], in_=yS[:, i, :])),
                     reads=["yS"], writes=["ybufS%d" % (t0 // AGW)], dma_sem="youtS")

            if T + 1 < (NT if stop > 1 else 0):
                load_h(T + 1)
            yield

        def attn_gen(T):
            t0 = T * 512
            QT = QTs[T % 2]
            zaT = zaTs[T % 2]
            par = T % 2
            steps = []
            for hd in range(2 if stop >= 1.8 else 0):
                for qq in range(2):
                    qi = 2 * T + qq
                    qo = qq * 256
                    nkt = 2 * qi + 2
                    for kt in range(0, nkt, 2):
                        steps.append((hd, qq, qi, qo, nkt, kt))

            def attn_front(idx, hd, qq, qi, qo, nkt, kt0):
                sb_i = idx % 2
                Sv = Sbig[:, sb_i * 1024:(sb_i + 1) * 1024].rearrange("p (t j q) -> p t j q", t=2, j=2)
                skey = "psS%d" % sb_i
                Pt = PT[idx % 2]
                pkey = "PT%d" % (idx % 2)
                diag = (kt0 == nkt - 2)
                for t in range(2):
                    kt = kt0 + t
                    c_lo = 128 if (diag and t == 1) else 0
                    kkey = "KT%d_%d" % (hd, kt // 4)
                    for j in range(2):
                        P.op("pe", (lambda e, j=j, t=t, kt=kt, c_lo=c_lo: e.matmul(
                            out=Sv[:, t, j, c_lo:256], lhsT=KT[:, hd, kt * 128:(kt + 1) * 128],
                            rhs=QT[:, hd, j, qo + c_lo:qo + 256], start=True, stop=(not diag))),
                             reads=[kkey, "QT%d_%d" % (par, hd)], writes=[skey])
                        if diag:
                            P.op("pe", (lambda e, j=j, t=t, c_lo=c_lo: e.matmul(
                                out=Sv[:, t, j, c_lo:c_lo + 128], lhsT=ident_b, rhs=negm_b,
                                start=False, stop=True)),
                                 reads=["cstb"], writes=[skey])
                if not diag:
                    P.op("act", (lambda e: e.activation(
                        out=Pt[:].rearrange("p t j q -> p (t j q)"),
                        in_=Sbig[:, sb_i * 1024:(sb_i + 1) * 1024], func=AF.Exp, scale=0.125)),
                         reads=[skey], writes=[pkey])
                else:
                    P.op("act", (lambda e: e.activation(
                        out=Pt[:, 0].rearrange("p j q -> p (j q)"),
                        in_=Sbig[:, sb_i * 1024:sb_i * 1024 + 512], func=AF.Exp, scale=0.125)),
                         reads=[skey], writes=[pkey])
                    P.op("act", (lambda e: e.activation(
                        out=Pt[:, 1, :, 128:256], in_=Sv[:, 1, :, 128:256], func=AF.Exp, scale=0.125)),
                         reads=[skey], writes=[pkey])

            def attn_back(idx, hd, qq, qi, qo, nkt, kt0):
                Pt = PT[idx % 2]
                pkey = "PT%d" % (idx % 2)
                Ob = banks[4].rearrange("p (j q) -> p j q", j=2)
                Lb = banks[5].rearrange("p (j q) -> p j q", j=2)
                diag = (kt0 == nkt - 2)
                for t in range(2):
                    kt = kt0 + t
                    c_lo = 128 if (diag and t == 1) else 0
                    if c_lo == 0:
                        Pt2 = Pt[:, t].rearrange("p j q -> p (j q)")
                        P.op("pe", (lambda e, kt=kt, Pt2=Pt2: e.matmul(
                            out=banks[4][:, :], lhsT=Vt[:, kt, 128 * hd:128 * hd + 128], rhs=Pt2,
                            start=(kt == 0), stop=False)),
                             reads=[pkey, "V%d" % kt], writes=["psO"])
                        P.op("pe", (lambda e, kt=kt, Pt2=Pt2: e.matmul(
                            out=banks[5][:, :], lhsT=ones_b, rhs=Pt2,
                            start=(kt == 0), stop=False)),
                             reads=[pkey, "cstb"], writes=["psL"])
                    else:
                        for j in range(2):
                            P.op("pe", (lambda e, j=j, kt=kt, t=t: e.matmul(
                                out=Ob[:, j, 128:256], lhsT=Vt[:, kt, 128 * hd:128 * hd + 128],
                                rhs=Pt[:, t, j, 128:256], start=False, stop=(j == 1))),
                                 reads=[pkey, "V%d" % kt], writes=["psO"])
                            P.op("pe", (lambda e, j=j, kt=kt, t=t: e.matmul(
                                out=Lb[:, j, 128:256], lhsT=ones_b, rhs=Pt[:, t, j, 128:256],
                                start=False, stop=(j == 1))),
                                 reads=[pkey, "cstb"], writes=["psL"])
                kt = kt0 + 1
                if kt != nkt - 1:
                    return None
                P.op("act", lambda e: e.activation(out=rl[:].rearrange("p j q -> p (j q)"), in_=banks[5][:, :],
                                                   func=AF.Ln), reads=["psL"], writes=["rl"])
                P.op("act", lambda e: e.activation(out=rl[:].rearrange("p j q -> p (j q)"),
                                                   in_=rl[:].rearrange("p j q -> p (j q)"), func=AF.Exp, scale=-1.0),
                     reads=["rl"], writes=["rl"])
                P.op("dve", lambda e: e.tensor_tensor(out=On[:].rearrange("p j q -> p (j q)"), in0=banks[4][:, :],
                                                      in1=rl[:].rearrange("p j q -> p (j q)"), op=ALU.mult),
                     reads=["psO", "rl"], writes=["On"])
                P.op("dve", lambda e: e.scalar_tensor_tensor(out=Od[:], in0=On[:, 1, :], scalar=neglam[:, 0:1],
                                                             in1=On[:, 0, :], op0=ALU.mult, op1=ALU.add),
                     reads=["On", "neglam"], writes=["Od"])
                P.op("dve", lambda e: e.tensor_tensor(out=sqb[:], in0=Od[:], in1=Od[:], op=ALU.mult),
                     reads=["Od"], writes=["sqb"])
                return lambda: attn_fin(hd, qo)

            def attn_fin(hd, qo):
                P.op("pe", (lambda e: e.matmul(out=banks[5][:, 0:256], lhsT=ones_b, rhs=sqb[:],
                                               start=True, stop=True)),
                     reads=["sqb", "cstb"], writes=["psL"])
                P.op("dve", (lambda e: e.tensor_scalar(
                    out=rstd[:], in0=banks[5][:, 0:256], scalar1=1.0 / 128.0, scalar2=EPS,
                    op0=ALU.mult, op1=ALU.add)), reads=["psL"], writes=["rstd"])
                P.op("act", lambda e: e.activation(out=rstd[:], in_=rstd[:], func=AF.Ln),
                     reads=["rstd"], writes=["rstd"])
                P.op("act", lambda e: e.activation(out=rstd[:], in_=rstd[:], func=AF.Exp, scale=-0.5),
                     reads=["rstd"], writes=["rstd"])
                P.op("dve", lambda e: e.scalar_tensor_tensor(out=rstd[:], in0=Od[:], scalar=anw[:, 0:1],
                                                             in1=rstd[:], op0=ALU.mult, op1=ALU.mult),
                     reads=["Od", "anw", "rstd"], writes=["rstd"])
                P.op("pool", (lambda e: e.tensor_tensor(
                    out=yA[:, hd, qo:qo + 256], in0=rstd[:], in1=zaT[:, hd, qo:qo + 256], op=ALU.mult)),
                     reads=["rstd", "zaT%d_%d" % (par, hd)], writes=["yA"])

            for idx, st_ in enumerate(steps):
                attn_front(idx, *st_)
                fin = attn_back(idx - 1, *steps[idx - 1]) if idx >= 1 else None
                yield (fin is not None)
                if fin is not None:
                    fin()
            if steps:
                fin = attn_back(len(steps) - 1, *steps[-1])
                if fin is not None:
                    fin()
            for hd in range(2 if stop >= 1.8 else 0):
                P.op("sp", (lambda e, hd=hd, t0=t0: e.dma_start(
                    out=ybufs[t0 // AGW].ap()[256 + 128 * hd:256 + 128 * hd + 128, t0 % AGW:t0 % AGW + 512],
                    in_=yA[:, hd, :])),
                     reads=["yA"], writes=["ybufA%d" % (t0 // AGW)], dma_sem="youtA")
            if stop >= 3 and (t0 + 512) % AGW == 0:
                a_i = t0 // AGW
                P.op("pool", (lambda e, a_i=a_i: e.collective_compute(
                    "AllGather", ALU.bypass, replica_groups=[[0, 1, 2, 3], [4, 5, 6, 7]],
                    ins=[ybufs[a_i].ap().opt()], outs=[ygaths[a_i].ap().opt()])),
                     reads=["ybufS%d" % a_i, "ybufA%d" % a_i], writes=["ygath%d" % a_i], dma_sem="cc%d" % a_i, inc=1)

            yield

        def drain(g):
            for _ in g:
                pass

        OTHER_YIELDS = 44
        NTr = NT if stop > 1 else 0
        if NTr:
            load_h(0)
            OTHER_YIELDS = sum(1 for _ in other_gen(0))
        for T in range(NTr):
            ag = attn_gen(T)
            n_attn = 8 * T + 6
            if T + 1 < NTr:
                og = other_gen(T + 1)
                n_other = OTHER_YIELDS
                done_o = 0
                alive = True
                extra = 0
                for k_, bnd in enumerate(ag):
                    if bnd:
                        extra += 1
                    want = min(n_other, ((k_ + 1) * n_other) // n_attn + 1 + extra)
                    while alive and done_o < want:
                        try:
                            next(og)
                            done_o += 1
                        except StopIteration:
                            alive = False
                if alive:
                    drain(og)
            else:
                drain(ag)

        if debug and stop >= 3:
            for a_i in range(NAG):
                P.op("sp", (lambda e, a_i=a_i: e.dma_start(out=dbg_d[:, a_i * AGW:(a_i + 1) * AGW],
                                                          in_=ygaths[a_i].ap())),
                     reads=["ygath%d" % a_i], writes=["dbg"], dma_sem="dbg")

        P.barrier()
        wout_sb = KT[:].rearrange("p h l -> p (h l)")
        big_ok = (2 * L >= 16 * 1024)
        if not big_ok:
            wout_t = sb("wout_t", [128, 16 * 1024], BF16)
            wout_sb = wout_t[:]
        wout_v = wout_sb[:, 0:16 * 1024].rearrange("p (c n) -> p c n", c=16)
        if big_ok:
            vflat = Vt[:].rearrange("p a b -> p (a b)")
            Yts = [vflat[:, (2 * i) * 16 * ST3:(2 * i + 1) * 16 * ST3].rearrange("p (c t) -> p c t", c=16)
                   for i in range(2)]
            Yns = [vflat[:, (2 * i + 1) * 16 * ST3:(2 * i + 2) * 16 * ST3].rearrange("p (c t) -> p c t", c=16)
                   for i in range(2)]
        else:
            Yts = [sb("Yt%d" % i, [128, 16, ST3], BF16)[:] for i in range(1)] * 2
            Yns = [sb("Yn%d" % i, [128, 16, ST3], BF16)[:] for i in range(1)] * 2
        nbuf3 = 2 if big_ok else 1
        sq3s = [xbc[:, i, 0:ST3] for i in range(4)]
        rs3 = etmp[:, 0:2 * ST3].rearrange("p (g t) -> p g t", g=2)
        x0f = xTs[0][:].rearrange("p a b -> p (a b)")
        x1f = xTs[1][:].rearrange("p a b -> p (a b)")
        lng = x0f[:, 0:1024]
        lnb = x0f[:, 1024:2048]
        pres = [x0f[:, 2048:3072], hT[:].rearrange("p a b -> p (a b)").bitcast(F32)[:, 0:1024]]
        junk = x0f[:, 3072:4096]
        xt = [x1f[:, 0:1024], x1f[:, 1024:2048]]
        ot = [x1f[:, 2048:3072], x1f[:, 3072:4096]]

        for c in range(16):
            P.op("pool", (lambda e, c=c: e.dma_start(out=wout_v[:, c, :], in_=wout_d[c])),
                 writes=["wout"], dma_sem="wout")
        P.op("sp", lambda e: e.dma_start(out=lng, in_=lng_d), writes=["lnp"], dma_sem="misc")
        P.op("sp", lambda e: e.dma_start(out=lnb, in_=lnb_d), writes=["lnp"], dma_sem="misc")

        dyn = {}
        NS3r = NS3 if stop >= 4 else 0

        def load_yt(S):
            b3 = S % nbuf3
            ygv = ygaths[S].ap().rearrange("(c p) t -> p c t", p=128)
            for c8 in range(2):
                P.op("sp", (lambda e, c8=c8, ygv=ygv, b3=b3: e.dma_start(
                    out=Yts[b3][:, 8 * c8:8 * c8 + 8, :],
                    in_=ygv[:, 8 * c8:8 * c8 + 8, bass.ds(dyn["roff"], ST3)])),
                     reads=["ygath%d" % S], writes=["Yt%d" % b3], dma_sem="yt%d" % b3)

        def norm_yt(S):
            b3 = S % nbuf3
            Yt, Yn = Yts[b3], Yns[b3]
            for g in range(2):
                bank, bkey = gen_bank()
                chunks = [8 * g + 4 * rr + m for rr in range(2) for m in range(2)]
                for n_i, c in enumerate(chunks):
                    P.op("dve", (lambda e, c=c, n_i=n_i, Yt=Yt: e.tensor_tensor(
                        out=sq3s[n_i], in0=Yt[:, c, :], in1=Yt[:, c, :], op=ALU.mult)),
                         reads=["Yt%d" % b3], writes=["sq3_%d" % n_i])
                for n_i, c in enumerate(chunks):
                    P.op("pe", (lambda e, bank=bank, n_i=n_i: e.matmul(
                        out=bank[:, 0:ST3], lhsT=ones_b, rhs=sq3s[n_i], start=(n_i == 0), stop=(n_i == 3))),
                         reads=["sq3_%d" % n_i, "cstb"], writes=[bkey])
                P.op("dve", (lambda e, bank=bank, g=g: e.tensor_scalar(
                    out=rs3[:, g, :], in0=bank[:, 0:ST3], scalar1=1.0 / 512.0, scalar2=EPS,
                    op0=ALU.mult, op1=ALU.add)), reads=[bkey], writes=["rs3"])
                P.op("act", (lambda e, g=g: e.activation(out=rs3[:, g, :], in_=rs3[:, g, :], func=AF.Ln)),
                     reads=["rs3"], writes=["rs3"])
                P.op("act", (lambda e, g=g: e.activation(out=rs3[:, g, :], in_=rs3[:, g, :], func=AF.Exp,
                                                         scale=-0.5)), reads=["rs3"], writes=["rs3"])
                for c in chunks:
                    P.op("dve", (lambda e, c=c, g=g, Yt=Yt, Yn=Yn: e.scalar_tensor_tensor(
                        out=Yn[:, c, :], in0=Yt[:, c, :], scalar=nw[:, c:c + 1], in1=rs3[:, g, :],
                        op0=ALU.mult, op1=ALU.mult)), reads=["Yt%d" % b3, "nw", "rs3"], writes=["Yn%d" % b3])

        def proj_ln(S, st):
            b3 = S % nbuf3
            Yt, Yn = Yts[b3], Yns[b3]
            gi = S * (ST3 // 128) + st
            xs = gi % 2
            pre = pres[xs]
            pk = "pre%d" % xs
            sm = small[:, 8 * xs:8 * xs + 8]
            sk = "small%d" % xs
            P.op("sp", (lambda e: e.dma_start(out=xt[xs], in_=xtok_d[gi * 128:(gi + 1) * 128, :])),
                 writes=["xt%d" % xs], dma_sem="xt%d" % xs)
            for half in range(2):
                bank, bkey = gen_bank()
                for c in range(16):
                    src = Yn if c % 4 < 2 else Yt
                    skey = ("Yn%d" if c % 4 < 2 else "Yt%d") % b3
                    P.op("pe", (lambda e, bank=bank, c=c, half=half, src=src: e.matmul(
                        out=bank[:, :], lhsT=src[:, c, st * 128:(st + 1) * 128],
                        rhs=wout_v[:, c, half * 512:(half + 1) * 512], start=(c == 0), stop=(c == 15))),
                         reads=[skey, "wout"], writes=[bkey])
                P.op("dve", (lambda e, bank=bank, half=half: e.tensor_tensor(
                    out=pre[:, half * 512:(half + 1) * 512], in0=bank[:, :],
                    in1=gate_bc[:, half * 512:(half + 1) * 512], op=ALU.mult)),
                     reads=[bkey, "gate_bc"], writes=[pk])
            P.op("dve", (lambda e: e.scalar_tensor_tensor(out=pre, in0=xt[xs], scalar=ALPHA, in1=pre,
                                                          op0=ALU.mult, op1=ALU.add)),
                 reads=[pk, "xt%d" % xs], writes=[pk])
            P.op("pool", lambda e: e.memset(sm[:, 0:2], 0.0), writes=[sk])
            P.op("act", lambda e: e.activation(out=junk, in_=pre, func=AF.Identity, accum_out=sm[:, 0:1]),
                 reads=[pk, sk], writes=["junk", sk])
            P.op("act", lambda e: e.activation(out=junk, in_=pre, func=AF.Square, accum_out=sm[:, 1:2]),
                 reads=[pk, sk], writes=["junk", sk])
            P.op("dve", lambda e: e.tensor_scalar(out=sm[:, 2:4], in0=sm[:, 0:2], scalar1=1.0 / 1024.0,
                                                  scalar2=None, op0=ALU.mult), reads=[sk], writes=[sk])
            P.op("dve", lambda e: e.tensor_tensor(out=sm[:, 4:5], in0=sm[:, 2:3], in1=sm[:, 2:3],
                                                  op=ALU.mult), reads=[sk], writes=[sk])
            P.op("dve", lambda e: e.tensor_tensor(out=sm[:, 5:6], in0=sm[:, 3:4], in1=sm[:, 4:5],
                                                  op=ALU.subtract), reads=[sk], writes=[sk])
            P.op("dve", lambda e: e.tensor_scalar(out=sm[:, 6:7], in0=sm[:, 5:6], scalar1=EPS, scalar2=None,
                                                  op0=ALU.add), reads=[sk], writes=[sk])
            P.op("act", lambda e: e.activation(out=sm[:, 6:7], in_=sm[:, 6:7], func=AF.Ln),
                 reads=[sk], writes=[sk])
            P.op("act", lambda e: e.activation(out=sm[:, 6:7], in_=sm[:, 6:7], func=AF.Exp, scale=-0.5),
                 reads=[sk], writes=[sk])
            P.op("dve", lambda e: e.scalar_tensor_tensor(out=sm[:, 7:8], in0=sm[:, 2:3], scalar=-1.0,
                                                         in1=sm[:, 6:7], op0=ALU.mult, op1=ALU.mult),
                 reads=[sk], writes=[sk])
            P.op("act", lambda e: e.activation(out=pre, in_=pre, func=AF.Identity, bias=sm[:, 7:8],
                                               scale=sm[:, 6:7]), reads=[pk, sk], writes=[pk])
            P.op("dve", lambda e: e.tensor_tensor(out=pre, in0=pre, in1=lng, op=ALU.mult),
                 reads=[pk, "lnp"], writes=[pk])
            P.op("pool", (lambda e: e.tensor_tensor(out=ot[xs], in0=pre, in1=lnb, op=ALU.add)),
                 reads=[pk, "lnp"], writes=["ot%d" % xs])
            P.op("sp", (lambda e: e.dma_start(out=out_d[gi * 128:(gi + 1) * 128, :], in_=ot[xs])),
                 reads=["ot%d" % xs], writes=["out%d" % gi], dma_sem="o%d" % xs)

        if NS3r:
            load_yt(0)
            norm_yt(0)
        for S in range(NS3r):
            if S + 1 < NS3r:
                load_yt(S + 1)
            for st in range(ST3 // 128):
                proj_ln(S, st)
                if st == 0 and S + 1 < NS3r:
                    norm_yt(S + 1)
        P.wait_all("sp")

        with nc.Block() as block:
            @block.tensor
            def _(e):
                P.replay("pe", e, sems, dma_sems)

            @block.scalar
            def _(e):
                P.replay("act", e, sems, dma_sems)

            @block.vector
            def _(e):
                P.replay("dve", e, sems, dma_sems)

            @block.gpsimd
            def _(e):
                P.replay("pool", e, sems, dma_sems)

            @block.sync
            def _(e):
                rank = e.partition_id() % 4
                dyn["roff"] = e.snap(rank * ST3, min_val=0, max_val=3 * ST3)
                P.replay("sp", e, sems, dma_sems)
    return nc


def make_consts():
    s = np.arange(128)[:, None]
    l = np.arange(128)[None, :]
    tri_incl = (s <= l).astype(np.float32)
    tri_su = (s > l).astype(np.float32)
    negm = np.where(s <= l, 0.0, NEG).astype(np.float32)
    ones = np.ones((128, 128), np.float32)
    ident = np.eye(128, dtype=np.float32)
    return np.ascontiguousarray(np.concatenate([tri_incl, tri_su, negm, ones, ident], axis=1))


def tok_idx(L, r):
    nag = max(L // 1024, 1)
    agw = L // nag
    st3 = agw // 4
    return np.concatenate([a * agw + r * st3 + np.arange(st3) for a in range(nag)])


def prep_inputs(inp, L):
    f = lambda a: np.ascontiguousarray(np.asarray(a, dtype=np.float32))
    x = f(inp["x"]); c = f(inp["c"])
    w_ada = f(inp["w_ada"])[0]; b_ada = f(inp["b_ada"])[0]; w_in = f(inp["w_in"])[0]
    conv_w = f(inp["conv_w"])[0]; conv_b = f(inp["conv_b"])[0]
    dt_bias = f(inp["dt_bias"])[0]; a_log = f(inp["a_log"])[0]; d_skip = f(inp["d_skip"])[0]
    ssd_norm_w = f(inp["ssd_norm_w"])[0]
    lq1 = f(inp["lambda_q1"])[0]; lk1 = f(inp["lambda_k1"])[0]
    lq2 = f(inp["lambda_q2"])[0]; lk2 = f(inp["lambda_k2"])[0]
    attn_norm_w = f(inp["attn_norm_w"])[0]; w_out = f(inp["w_out"])[0]
    ln_g = f(inp["ln_g"])[0]; ln_b = f(inp["ln_b"])[0]
    LQ = L // 4
    cst = make_consts()
    wada_r = np.ascontiguousarray(w_ada.reshape(8, 128, 3072))
    bada_col = np.ascontiguousarray(b_ada.reshape(24, 128).T)
    bgate = np.ascontiguousarray(np.tile(b_ada[None, 2048:3072], (128, 1)))
    lamv = np.ascontiguousarray(np.tile(np.concatenate([lq1, lk1, lq2, lk2])[None, :], (128, 1)))
    lng = np.ascontiguousarray(np.tile(ln_g[None, :], (128, 1)))
    lnb = np.ascontiguousarray(np.tile(ln_b[None, :], (128, 1)))
    anw = np.ascontiguousarray(attn_norm_w.reshape(128, 1))
    rows = []
    for r in range(4):
        rows += list(range(256 * r, 256 * r + 256)) + list(range(1024 + 256 * r, 1024 + 256 * r + 256))
    rows = np.array(rows)
    wout_r = np.ascontiguousarray(w_out[rows].reshape(16, 128, 1024))
    nw = np.ones((128, 16), np.float32)
    for r in range(4):
        for m in range(2):
            nw[:, 4 * r + m] = ssd_norm_w[256 * r + 128 * m:256 * r + 128 * m + 128]
    maps = []
    for core in range(8):
        b, hp = core // 4, core % 4
        g = hp // 2
        cols = np.concatenate([
            np.arange(256 * hp, 256 * hp + 256),
            OFF_XBC + np.arange(256 * hp, 256 * hp + 256),
            OFF_XBC + 1024 + 128 * g + np.arange(128),
            OFF_XBC + 1280 + 128 * g + np.arange(128),
            OFF_Q + 256 * hp + np.arange(256),
            OFF_K + 256 * hp + np.arange(256),
            OFF_ZA + 256 * hp + np.arange(256),
            OFF_V + 256 * hp + np.arange(256),
            OFF_DT + 4 * hp + np.arange(4),
        ])
        win = np.zeros((1024, NCOLS), np.float32)
        win[:, :cols.size] = w_in[:, cols]
        chans = [256 * hp + np.arange(128), 256 * hp + 128 + np.arange(128),
                 1024 + 128 * g + np.arange(128), 1280 + 128 * g + np.arange(128)]
        convw = np.zeros((128, 16), np.float32)
        convb = np.zeros((128, 4), np.float32)
        for ch in range(4):
            convw[:, 4 * ch:4 * ch + 4] = conv_w[:, chans[ch]].T
            convb[:, ch] = conv_b[chans[ch]]
        hsel = np.arange(4 * hp, 4 * hp + 4)
        dtb = np.tile(np.tile(dt_bias[hsel], 4)[None, :], (128, 1))
        alog = np.tile(np.tile(a_log[hsel], 4)[None, :], (128, 1))
        dskip = np.zeros((128, 2), np.float32)
        for i in range(2):
            dskip[:, i] = np.repeat(d_skip[hsel[2 * i:2 * i + 2]], 64)
        maps.append({
            "xT": np.ascontiguousarray(x[b, :L].T.reshape(8, 128, L)),
            "xtok": np.ascontiguousarray(x[b, tok_idx(L, hp), :]),
            "ccol": np.ascontiguousarray(c[b].reshape(8, 128).T),
            "wada": wada_r, "bada_col": bada_col, "bgate": bgate,
            "win": np.ascontiguousarray(win.reshape(8, 128, NCOLS)),
            "convw": convw, "convb": convb,
            "dtb": np.ascontiguousarray(dtb.astype(np.float32)),
            "alog": np.ascontiguousarray(alog.astype(np.float32)),
            "dskip": dskip, "lamv": lamv, "anw": anw, "nw": nw, "wout": wout_r,
            "lng": lng, "lnb": lnb, "cst": cst,
        })
    return maps


def run(inp, L, debug=False, stop=9):
    nc = build_program(L, debug=debug, stop=stop)
    maps = prep_inputs(inp, L)
    res = run_bass_kernel_spmd(nc, maps, core_ids=list(range(8)))
    LQ = L // 4
    out = np.zeros((2, L, 1024), np.float32)
    for core in range(8):
        b, hp = core // 4, core % 4
        out[b, tok_idx(L, hp), :] = np.asarray(res.results[core]["out"])
    if debug:
        return out, [np.asarray(res.results[core]["dbg"]) for core in range(8)]
    return out


def kernel(**inputs):
    return run(inputs, SEQ)
```

```python
import math
from contextlib import ExitStack

import numpy as np
import concourse.bass as bass
import concourse.mybir as mybir
from concourse.bass_utils import run_bass_kernel_spmd

F32 = mybir.dt.float32
BF16 = mybir.dt.bfloat16
AF = mybir.ActivationFunctionType
ALU = mybir.AluOpType
AX = mybir.AxisListType

D_MODEL = 1024
SEQ = 8192
D_SSD = 1024
D_XBC = 1536
OFF_XBC = 1024
OFF_DT = OFF_XBC + D_XBC
OFF_Q = OFF_DT + 16
OFF_K = OFF_Q + 1024
OFF_V = OFF_K + 1024
OFF_ZA = OFF_V + 1024
D_PROJ = OFF_ZA + 1024
ALPHA = 2.0 ** 0.25
EPS = 1e-5
LAMBDA_INIT = 0.8 - 0.6 * math.exp(0.0)
NCOLS = 1800
C_V = 1536
NEG = -30000.0

ENGS = ["pe", "act", "dve", "pool", "sp"]


class Prog:
    def __init__(self):
        self.streams = {e: [] for e in ENGS}
        self.nops = {e: 0 for e in ENGS}
        self.seen = {e: {} for e in ENGS}
        self.state = {}
        self.dma_val = {}
        self.milestones = {e: set() for e in ENGS}

    def _need(self, needed, tok):
        k = tok[0]
        if k not in needed or needed[k][1] < tok[1]:
            needed[k] = tok

    def _emit_waits(self, eng, needed):
        for k, tok in needed.items():
            if self.seen[eng].get(k, 0) >= tok[1]:
                continue
            self.seen[eng][k] = tok[1]
            self.streams[eng].append(("wait", tok))
            if tok[2] == "eng":
                self.milestones[k].add(tok[1])

    def op(self, eng, fn, reads=(), writes=(), dma_sem=None, inc=16):
        needed = {}
        for k in reads:
            st = self.state.get(k)
            if st and st["w"]:
                self._need(needed, st["w"])
        for k in writes:
            st = self.state.get(k)
            if st:
                if st["w"]:
                    t = st["w"]
                    if not (t[2] == "eng" and t[0] == eng) and not (t[2] == "dma" and t[0] == dma_sem):
                        self._need(needed, t)
                for t in st["r"].values():
                    if not (t[2] == "eng" and t[0] == eng):
                        self._need(needed, t)
        if eng == "pe":
            needed.pop("pe", None)
        self._emit_waits(eng, needed)
        if dma_sem is None:
            self.nops[eng] += 1
            tok = (eng, self.nops[eng], "eng")
            self.streams[eng].append(("op", fn, self.nops[eng]))
        else:
            self.dma_val[dma_sem] = self.dma_val.get(dma_sem, 0) + inc
            tok = (dma_sem, self.dma_val[dma_sem], "dma")
            self.streams[eng].append(("dma", fn, dma_sem, inc))
        for k in reads:
            st = self.state.setdefault(k, {"w": None, "r": {}})
            st["r"][tok[0]] = tok
        for k in writes:
            self.state[k] = {"w": tok, "r": {}}
        return tok

    def wait_all(self, eng):
        needed = {}
        for e in ENGS:
            if e != eng and self.nops[e] > 0:
                self._need(needed, (e, self.nops[e], "eng"))
        for s, v in self.dma_val.items():
            self._need(needed, (s, v, "dma"))
        self._emit_waits(eng, needed)

    def barrier(self):
        snap_ops = dict(self.nops)
        snap_dma = dict(self.dma_val)
        for eng in ENGS:
            needed = {}
            for e in ENGS:
                if e != eng and snap_ops[e] > 0:
                    self._need(needed, (e, snap_ops[e], "eng"))
            for s, v in snap_dma.items():
                if s.startswith("cc"):
                    continue
                self._need(needed, (s, v, "dma"))
            self._emit_waits(eng, needed)

    def replay(self, eng, engine_obj, sems, dma_sems):
        ms = sorted(self.milestones[eng])
        rank = {v: i + 1 for i, v in enumerate(ms)}
        for item in self.streams[eng]:
            if item[0] == "wait":
                tok = item[1]
                if tok[2] == "eng":
                    msl = sorted(self.milestones[tok[0]])
                    import bisect
                    val = bisect.bisect_right(msl, tok[1])
                    engine_obj.wait_ge(sems[tok[0]], val)
                else:
                    engine_obj.wait_ge(dma_sems[tok[0]], tok[1])
            elif item[0] == "op":
                inst = item[1](engine_obj)
                if item[2] in rank:
                    inst.then_inc(sems[eng], 1)
            else:
                inst = item[1](engine_obj)
                if item[3] == 16:
                    inst.then_inc(dma_sems[item[2]], 16)
                else:
                    inst.then_inc(dma_sems[item[2]])


def build_program(L, debug=False, stop=9):
    NT = L // 512
    LQ = L // 4
    NAG = max(L // 1024, 1)
    AGW = L // NAG
    NS3 = NAG
    ST3 = AGW // 4
    nc = bass.Bass("TRN2", target_bir_lowering=False)
    P = Prog()

    def din(name, shape, dt=F32):
        return nc.dram_tensor(name, list(shape), dt, kind="ExternalInput").ap()

    xT_d = din("xT", [8, 128, L])
    xtok_d = din("xtok", [LQ, 1024])
    ccol_d = din("ccol", [128, 8])
    wada_d = din("wada", [8, 128, 3072])
    bada_col_d = din("bada_col", [128, 24])
    bgate_d = din("bgate", [128, 1024])
    win_d = din("win", [8, 128, NCOLS])
    convw_d = din("convw", [128, 16])
    convb_d = din("convb", [128, 4])
    dtb_d = din("dtb", [128, 16])
    alog_d = din("alog", [128, 16])
    dskip_d = din("dskip", [128, 2])
    lamv_d = din("lamv", [128, 256])
    anw_d = din("anw", [128, 1])
    nw_d = din("nw", [128, 16])
    wout_d = din("wout", [16, 128, 1024])
    lng_d = din("lng", [128, 1024])
    lnb_d = din("lnb", [128, 1024])
    cst_d = din("cst", [128, 5 * 128])
    out_d = nc.dram_tensor("out", [LQ, 1024], F32, kind="ExternalOutput").ap()
    ybufs = [nc.dram_tensor("ybuf%d" % a, [512, AGW], BF16) for a in range(NAG)]
    ygaths = [nc.dram_tensor("ygath%d" % a, [2048, AGW], BF16) for a in range(NAG)]
    if debug:
        dbg_d = nc.dram_tensor("dbg", [2048, L], BF16, kind="ExternalOutput").ap()

    es = ExitStack()
    with es:
        def sb(name, shape, dt=F32):
            return es.enter_context(nc.sbuf_tensor(name, list(shape), dt))

        sems = {e: es.enter_context(nc.semaphore("s_" + e)) for e in ENGS}
        dma_names = ["xA0", "xA1", "xT0", "xT1", "cst", "win", "youtS", "youtA", "wout", "yt0", "yt1", "xt0", "xt1",
                     "o0", "o1", "misc", "dbg"]
        dma_names += ["cc%d" % a for a in range(NAG)]
        dma_sems = {n: es.enter_context(nc.semaphore("d_" + n)) for n in dma_names}

        xTs = [sb("xTs%d" % i, [128, 8, 512]) for i in range(2)]
        hT = sb("hT", [128, 8, 512], BF16)
        win_sb = sb("win_sb", [128, 8, NCOLS], BF16)
        KT = sb("KT", [128, 2, L], BF16)
        Vt = sb("Vt", [128, L // 128, 256], BF16)
        QTs = [sb("QT%d" % i, [128, 2, 2, 512], BF16) for i in range(2)]
        zaTs = [sb("zaT%d" % i, [128, 2, 512]) for i in range(2)]
        zsT = sb("zsT", [128, 2, 512])
        ubuf = sb("ubuf", [128, 4, 515])
        cacc = sb("cacc", [128, 2, 512])
        xbc = sb("xbc", [128, 4, 512], BF16)
        etmp = sb("etmp", [128, 512])
        cbc = cacc[:, 0:2, :].rearrange("p a (b c) -> p (a b) c", c=128)
        dtraw = sb("dtraw", [128, 16])
        dtv = sb("dtv", [128, 16])
        adt = sb("adt", [128, 16])
        arep = sb("arep", [128, 16])
        dtb = sb("dtb_sb", [128, 16])
        atok = sb("atok", [128, 8])
        dtw = sb("dtw", [128, 4])
        adtbc = sb("adtbc", [128, 4, 128])
        Dm = sb("Dm", [128, 4, 128])
        LT = sb("LT", [128, 4, 128])
        Erow = sb("Erow", [128, 4, 128])
        MT = sb("MT", [128, 4, 128], BF16)
        CeT = sb("CeT", [128, 4, 128], BF16)
        xbtok = sb("xbtok", [128, 384], BF16)
        xdt = sb("xdt", [128, 256], BF16)
        xdtw = sb("xdtw", [128, 256], BF16)
        Hs = sb("Hs", [128, 256])
        Hbf = sb("Hbf", [128, 256], BF16)
        ytmp = sb("ytmp", [128, 128])
        yS = sb("yS", [128, 2, 512], BF16)
        yA = sb("yA", [128, 2, 512], BF16)
        PT = [sb("PT%d" % i, [128, 2, 2, 256], BF16) for i in range(2)]
        rl = sb("rl", [128, 2, 256])
        On = sb("On", [128, 2, 256])
        Od = sb("Od", [128, 256])
        sqb = sb("sqb", [128, 256], BF16)
        rstd = sb("rstd", [128, 256])
        lamv = On[:].rearrange("p j q -> p (j q)")[:, 0:256]
        lamt = Od[:, 0:128]
        cst = sb("cst_sb", [128, 640])
        cstb = sb("cstb", [128, 640], BF16)
        ccol = sb("ccol_sb", [128, 8])
        modT = sb("modT", [128, 24])
        badac = sb("badac", [128, 24])
        gate_bc = sb("gate_bc", [128, 1024])
        convw = sb("convw_sb", [128, 16])
        convb = sb("convb_sb", [128, 4])
        dskip = sb("dskip_sb", [128, 2])
        lams = sb("lams", [128, 4])
        neglam = sb("neglam", [128, 1])
        anw = sb("anw_sb", [128, 1])
        nw = sb("nw_sb", [128, 16])
        small = sb("small", [128, 16])

        tri_incl = cst[:, 0:128]
        tri_su = cst[:, 128:256]
        negmT = cst[:, 256:384]
        ones_f = cst[:, 384:512]
        ident_b = cstb[:, 512:640]
        ones_b = cstb[:, 384:512]
        triT_b = cstb[:, 0:128]
        negm_b = cstb[:, 256:384]

        Sbig = es.enter_context(nc.psum_tensor("Sbig", [128, 2048], F32))
        banks = [None] * 4 + [es.enter_context(nc.psum_tensor("bank%d" % i, [128, 512], F32)) for i in range(4, 8)]
        gen_state = {"i": 0}

        def gen_bank():
            i = 6 + (gen_state["i"] % 2)
            gen_state["i"] += 1
            return banks[i], "ps%d" % i

        P.op("sp", lambda e: e.dma_start(out=cst[:], in_=cst_d), writes=["cst"], dma_sem="cst")
        misc_keys = []
        for (t_sb, t_d, key) in [(ccol, ccol_d, "ccol"), (badac, bada_col_d, "badac"),
                                 (convw, convw_d, "convw"), (convb, convb_d, "convb"),
                                 (dtb, dtb_d, "dtb"), (arep, alog_d, "arep"),
                                 (dskip, dskip_d, "dskip"), (lamv, lamv_d, "lamv"),
                                 (anw, anw_d, "anw"), (nw, nw_d, "nw"),
                                 (gate_bc, bgate_d, "gate_bc")]:
            misc_tok = P.op("sp", (lambda e, a=t_sb, b=t_d, key=key: e.dma_start(out=(a if key == "lamv" else a[:]), in_=b)),
                            writes=[key], dma_sem="misc")
            misc_keys.append(key)
        for key in misc_keys:
            P.state[key]["w"] = misc_tok
        for kc in range(8):
            P.op("pool", (lambda e, kc=kc: e.dma_start(out=win_sb[:, kc, :], in_=win_d[kc])),
                 writes=["win"], dma_sem="win")
        P.op("dve", lambda e: e.tensor_copy(out=cstb[:], in_=cst[:]), reads=["cst"], writes=["cstb"])
        P.op("act", lambda e: e.activation(out=arep[:], in_=arep[:], func=AF.Exp),
             reads=["arep"], writes=["arep"])
        P.op("dve", lambda e: e.tensor_scalar(out=arep[:], in0=arep[:], scalar1=-1.0, scalar2=None,
                                              op0=ALU.mult), reads=["arep"], writes=["arep"])
        P.op("dve", lambda e: e.tensor_tensor(out=lamt[:, 0:64], in0=lamv[:, 0:64], in1=lamv[:, 64:128],
                                              op=ALU.mult), reads=["lamv"], writes=["lamt0"])
        P.op("dve", lambda e: e.tensor_tensor(out=lamt[:, 64:128], in0=lamv[:, 128:192],
                                              in1=lamv[:, 192:256], op=ALU.mult),
             reads=["lamv"], writes=["lamt1"])
        P.op("dve", lambda e: e.tensor_reduce(out=lams[:, 0:1], in_=lamt[:, 0:64], axis=AX.X, op=ALU.add),
             reads=["lamt0"], writes=["lams0"])
        P.op("dve", lambda e: e.tensor_reduce(out=lams[:, 1:2], in_=lamt[:, 64:128], axis=AX.X, op=ALU.add),
             reads=["lamt1"], writes=["lams1"])
        P.op("act", lambda e: e.activation(out=lams[:, 2:4], in_=lams[:, 0:2], func=AF.Exp),
             reads=["lams0", "lams1"], writes=["lams2"])
        P.op("dve", lambda e: e.tensor_tensor(out=neglam[:], in0=lams[:, 3:4], in1=lams[:, 2:3],
                                              op=ALU.subtract), reads=["lams2"], writes=["neglam"])
        P.op("dve", lambda e: e.tensor_scalar(out=neglam[:], in0=neglam[:], scalar1=-LAMBDA_INIT,
                                              scalar2=None, op0=ALU.add), reads=["neglam"], writes=["neglam"])
        P.op("dve", lambda e: e.tensor_scalar(out=anw[:], in0=anw[:], scalar1=1.0 - LAMBDA_INIT,
                                              scalar2=None, op0=ALU.mult), reads=["anw"], writes=["anw"])
        for kc in range(8):
            P.op("dve", (lambda e, kc=kc: e.tensor_scalar(out=cbc[:, kc, :], in0=ones_f,
                                                          scalar1=ccol[:, kc:kc + 1], scalar2=None,
                                                          op0=ALU.mult)),
                 reads=["cst", "ccol"], writes=["cbc"])
        for cb in range(6):
            slot = cb % 2
            for kc in range(8):
                P.op("sp" if kc % 2 == 0 else "act", (lambda e, cb=cb, slot=slot, kc=kc: e.dma_start(
                    out=xTs[slot][:, kc, :], in_=wada_d[kc, :, cb * 512:(cb + 1) * 512])),
                     writes=[("xTs%d" if kc % 2 == 0 else "xTa%d") % slot],
                     dma_sem=("xT%d" if kc % 2 == 0 else "xA%d") % slot)
            if cb < 4:
                bank, bkey = gen_bank()
                for j4 in range(4):
                    j = cb * 4 + j4
                    for kc in range(8):
                        P.op("pe", (lambda e, bank=bank, j4=j4, kc=kc, slot=slot: e.matmul(
                            out=bank[:, j4:j4 + 1], lhsT=xTs[slot][:, kc, j4 * 128:(j4 + 1) * 128],
                            rhs=ccol[:, kc:kc + 1], start=(kc == 0), stop=(kc == 7))),
                             reads=[("xTs%d" if kc % 2 == 0 else "xTa%d") % slot, "ccol"], writes=[bkey])
                P.op("dve", (lambda e, bank=bank, cb=cb: e.tensor_tensor(
                    out=modT[:, cb * 4:cb * 4 + 4], in0=bank[:, 0:4], in1=badac[:, cb * 4:cb * 4 + 4],
                    op=ALU.add)), reads=[bkey, "badac"], writes=["modT"])
            else:
                bank, bkey = gen_bank()
                for kc in range(8):
                    P.op("pe", (lambda e, bank=bank, kc=kc, slot=slot: e.matmul(
                        out=bank[:, :], lhsT=cbc[:, kc, :], rhs=xTs[slot][:, kc, :],
                        start=(kc == 0), stop=(kc == 7))),
                         reads=[("xTs%d" if kc % 2 == 0 else "xTa%d") % slot, "cbc"], writes=[bkey])
                c0 = (cb - 4) * 512
                P.op("dve", (lambda e, bank=bank, c0=c0: e.tensor_tensor(
                    out=gate_bc[:, c0:c0 + 512], in0=bank[:, :], in1=gate_bc[:, c0:c0 + 512], op=ALU.add)),
                     reads=[bkey, "gate_bc"], writes=["gate_bc"])
        P.op("dve", lambda e: e.tensor_scalar(out=modT[:, 8:16], in0=modT[:, 8:16], scalar1=1.0,
                                              scalar2=None, op0=ALU.add), reads=["modT"], writes=["modT"])
        P.op("pool", lambda e: e.memset(ubuf[:], 0.0), writes=["ubuf"])
        P.op("pool", lambda e: e.memset(Hs[:], 0.0), writes=["Hs"])
        P.op("pool", lambda e: e.memset(Hbf[:], 0.0), writes=["Hbf"])
        for qi_ in range(2):
            P.op("pool", (lambda e, qi_=qi_: e.memset(QTs[qi_][:], 0.0)), writes=["QT%d_0" % qi_, "QT%d_1" % qi_])

        def silu_to(src_ap, src_keys, dst_ap, dst_keys, shape_cols, dve_eng="dve"):
            P.op("act", lambda e: e.activation(out=etmp[:, 0:shape_cols], in_=src_ap, func=AF.Exp, scale=-1.0),
                 reads=src_keys, writes=["etmp"])
            P.op("dve", lambda e: e.tensor_scalar(out=etmp[:, 0:shape_cols], in0=etmp[:, 0:shape_cols],
                                                  scalar1=1.0, scalar2=None, op0=ALU.add),
                 reads=["etmp"], writes=["etmp"])
            P.op("dve", lambda e: e.reciprocal(out=etmp[:, 0:shape_cols], in_=etmp[:, 0:shape_cols]),
                 reads=["etmp"], writes=["etmp"])
            P.op("dve", lambda e: e.tensor_tensor(out=dst_ap, in0=src_ap, in1=etmp[:, 0:shape_cols],
                                                  op=ALU.mult),
                 reads=src_keys + ["etmp"], writes=dst_keys)

        def load_h(T):
            t0 = T * 512
            slot = T % 2
            xs_key = "xTs%d" % slot
            for kc in range(8):
                P.op("sp", (lambda e, kc=kc, slot=slot, t0=t0: e.dma_start(
                    out=xTs[slot][:, kc, :], in_=xT_d[kc, :, t0:t0 + 512])),
                     writes=[xs_key, "xTa%d" % slot], dma_sem="xT%d" % slot)
            for kc in range(8):
                P.op("dve", (lambda e, kc=kc, slot=slot: e.tensor_scalar(
                    out=hT[:, kc, :], in0=xTs[slot][:, kc, :], scalar1=modT[:, 8 + kc:9 + kc],
                    scalar2=modT[:, kc:kc + 1], op0=ALU.mult, op1=ALU.add)),
                     reads=[xs_key, "modT"], writes=["hT%d" % kc])

        def other_gen(T):
            t0 = T * 512
            slot = T % 2
            xs_key = "xTs%d" % slot
            QT = QTs[T % 2]
            zaT = zaTs[T % 2]
            par = T % 2
            hkeys = ["hT%d" % kc for kc in range(8)]
            wkeys = ["win"] * 8

            def inproj(g):
                bank, bkey = gen_bank()
                for kc in range(8):
                    P.op("pe", (lambda e, bank=bank, kc=kc, g=g: e.matmul(
                        out=bank[:, :], lhsT=win_sb[:, kc, g * 128:(g + 1) * 128], rhs=hT[:, kc, :],
                        start=(kc == 0), stop=(kc == 7))),
                         reads=[hkeys[kc], wkeys[kc]], writes=[bkey])
                return bank, bkey

            for hd in range(2):
                bank, bkey = inproj(8 + hd)
                P.op("dve", (lambda e, bank=bank, hd=hd, t0=t0: e.tensor_copy(
                    out=KT[:, hd, t0:t0 + 512], in_=bank[:, :])),
                     reads=[bkey], writes=["KT%d_%d" % (hd, T)])
                yield
            for hd in range(2):
                bank, bkey = inproj(6 + hd)
                for j in range(2):
                    P.op("dve", (lambda e, bank=bank, hd=hd, j=j: e.tensor_copy(
                        out=QT[64 * j:64 * j + 64, hd, j, :], in_=bank[64 * j:64 * j + 64, :])),
                         reads=[bkey], writes=["QT%d_%d" % (par, hd)])
                yield
            for s in range(4):
                bank, bkey = gen_bank()
                for kc in range(8):
                    P.op("pe", (lambda e, bank=bank, kc=kc, s=s: e.matmul(
                        out=bank[:, 0:260], lhsT=hT[:, kc, s * 128:(s + 1) * 128],
                        rhs=win_sb[:, kc, C_V:C_V + 260], start=(kc == 0), stop=(kc == 7))),
                         reads=[hkeys[kc], wkeys[kc]], writes=[bkey])
                kt = T * 4 + s
                P.op("dve", (lambda e, bank=bank, kt=kt: e.tensor_copy(out=Vt[:, kt, :], in_=bank[:, 0:256])),
                     reads=[bkey], writes=["V%d" % kt])
                P.op("dve", (lambda e, bank=bank, s=s: e.tensor_copy(out=dtraw[:, 4 * s:4 * s + 4],
                                                                     in_=bank[:, 256:260])),
                     reads=[bkey], writes=["dtraw"])
                yield
            if stop >= 1.6:
                P.op("dve", lambda e: e.tensor_tensor(out=dtv[:], in0=dtraw[:], in1=dtb[:], op=ALU.add),
                     reads=["dtraw", "dtb"], writes=["dtv"])
                P.op("act", lambda e: e.activation(out=dtv[:], in_=dtv[:], func=AF.Exp), reads=["dtv"], writes=["dtv"])
                P.op("act", lambda e: e.activation(out=dtv[:], in_=dtv[:], func=AF.Ln, bias=1.0),
                     reads=["dtv"], writes=["dtv"])
                P.op("dve", lambda e: e.tensor_tensor(out=adt[:], in0=dtv[:], in1=arep[:], op=ALU.mult),
                     reads=["dtv", "arep"], writes=["adt"])

            yield
            for ch in range(4 if stop >= 1.4 else 0):
                bank, bkey = inproj(2 + ch)
                P.op("act", (lambda e, bank=bank, ch=ch: e.activation(out=ubuf[:, ch, 3:515], in_=bank[:, :],
                                                                      func=AF.Copy)),
                     reads=[bkey], writes=["ubuf%d" % ch])
                P.op("dve", (lambda e, ch=ch: e.tensor_scalar(
                    out=cacc[:, ch % 2, :], in0=ubuf[:, ch, 0:512], scalar1=convw[:, 4 * ch:4 * ch + 1],
                    scalar2=convb[:, ch:ch + 1], op0=ALU.mult, op1=ALU.add)),
                     reads=["ubuf%d" % ch, "ubuf", "convw", "convb"], writes=["cacc%d" % (ch % 2), "cbc"])
                for k in range(1, 4):
                    P.op("dve", (lambda e, ch=ch, k=k: e.scalar_tensor_tensor(
                        out=cacc[:, ch % 2, :], in0=ubuf[:, ch, k:k + 512],
                        scalar=convw[:, 4 * ch + k:4 * ch + k + 1], in1=cacc[:, ch % 2, :],
                        op0=ALU.mult, op1=ALU.add)),
                         reads=["ubuf%d" % ch, "cacc%d" % (ch % 2)], writes=["cacc%d" % (ch % 2)])
                P.op("pool", (lambda e, ch=ch: e.tensor_copy(out=ubuf[:, ch, 0:3], in_=ubuf[:, ch, 512:515])),
                     reads=["ubuf%d" % ch], writes=["ubuf%d" % ch])
                silu_to(cacc[:, ch % 2, :], ["cacc%d" % (ch % 2)], xbc[:, ch, :], ["xbc%d" % ch], 512)
                yield

            for hd in range(2 if stop >= 1.4 else 0):
                bank, bkey = inproj(10 + hd)
                silu_to(bank[:, :], [bkey], zaT[:, hd, :], ["zaT%d_%d" % (par, hd)], 512)
                yield
            for i in range(2 if stop >= 1.4 else 0):
                bank, bkey = inproj(i)
                silu_to(bank[:, :], [bkey], zsT[:, i, :], ["zsT%d" % i], 512)
                yield
            for s in range(4 if stop >= 1.6 else 0):
                o = s * 128
                bk7, key7 = gen_bank()
                P.op("pe", (lambda e, s=s, bk7=bk7: e.matmul(out=bk7[:, 0:4], lhsT=tri_incl, rhs=adt[:, 4 * s:4 * s + 4],
                                                    start=True, stop=True)),
                     reads=["cst", "adt"], writes=[key7])
                P.op("pe", (lambda e, s=s, bk7=bk7: e.matmul(out=bk7[:, 4:8], lhsT=tri_su, rhs=adt[:, 4 * s:4 * s + 4],
                                                    start=True, stop=True)),
                     reads=["cst", "adt"], writes=[key7])
                P.op("pe", (lambda e, o=o, bk7=bk7: e.matmul(out=bk7[:, 128:256], lhsT=xbc[:, 2, o:o + 128],
                                                             rhs=xbc[:, 3, o:o + 128], start=True, stop=True)),
                     reads=["xbc2", "xbc3"], writes=[key7])
                P.op("dve", (lambda e, bk7=bk7: e.tensor_copy(out=atok[:], in_=bk7[:, 0:8])),
                     reads=[key7], writes=["atok"])
                P.op("act", lambda e: e.activation(out=dtw[:], in_=atok[:, 4:8], func=AF.Exp),
                     reads=["atok"], writes=["dtw"])
                P.op("dve", (lambda e, s=s: e.tensor_tensor(out=dtw[:], in0=dtw[:], in1=dtv[:, 4 * s:4 * s + 4],
                                                            op=ALU.mult)),
                     reads=["dtw", "dtv"], writes=["dtw"])
                yield
                for h in range(4):
                    P.op("dve", (lambda e, h=h, s=s: e.tensor_scalar(
                        out=adtbc[:, h, :], in0=ones_f, scalar1=adt[:, 4 * s + h:4 * s + h + 1], scalar2=None,
                        op0=ALU.mult)), reads=["cst", "adt"], writes=["adtbc%d" % h])
                yield
                yield
                yield
                bk6, key6 = gen_bank()
                for h in range(4):
                    P.op("pe", (lambda e, h=h, bk6=bk6: e.matmul(out=bk6[:, h * 128:(h + 1) * 128],
                                                        lhsT=adtbc[:, h, :], rhs=tri_incl, start=True, stop=True)),
                         reads=["adtbc%d" % h, "cst"], writes=[key6])
                yield
                for h in range(4):
                    P.op("dve", (lambda e, h=h, bk6=bk6: e.scalar_tensor_tensor(
                        out=Dm[:, h, :], in0=bk6[:, h * 128:(h + 1) * 128], scalar=atok[:, h:h + 1],
                        in1=negmT, op0=ALU.subtract, op1=ALU.add)),
                         reads=[key6, "atok", "cst"], writes=["Dm"])
                P.op("act", lambda e: e.activation(out=LT[:].rearrange("p h l -> p (h l)"),
                                                   in_=Dm[:].rearrange("p h l -> p (h l)"), func=AF.Exp),
                     reads=["Dm"], writes=["LT"])
                P.op("act", (lambda e, bk6=bk6: e.activation(out=Erow[:].rearrange("p h l -> p (h l)"),
                                                             in_=bk6[:, :], func=AF.Exp)),
                     reads=[key6], writes=["Erow"])
                yield
                for h in range(4):
                    P.op("dve", (lambda e, h=h, bk7=bk7: e.tensor_tensor(out=MT[:, h, :], in0=LT[:, h, :],
                                                                in1=bk7[:, 128:256], op=ALU.mult)),
                         reads=["LT", key7], writes=["MT"])
                    P.op("dve", (lambda e, h=h, o=o: e.tensor_tensor(out=CeT[:, h, :], in0=Erow[:, h, :],
                                                                      in1=xbc[:, 3, o:o + 128], op=ALU.mult)),
                         reads=["Erow", "xbc3"], writes=["CeT"])
                yield
                bank, bkey = gen_bank()
                bview = bank.bitcast(BF16)
                for i in range(3):
                    P.op("pe", (lambda e, bview=bview, i=i, o=o: e.transpose(
                        out=bview[:, i * 128:(i + 1) * 128], in_=xbc[:, i, o:o + 128], identity=ident_b)),
                         reads=["xbc%d" % i, "cstb"], writes=[bkey])
                P.op("dve", (lambda e, bview=bview: e.tensor_copy(out=xbtok[:], in_=bview[:, 0:384])),
                     reads=[bkey], writes=["xbtok"])
                for h in range(4):
                    P.op("dve", (lambda e, h=h, s=s: e.tensor_scalar(
                        out=xdt[:, 64 * h:64 * h + 64], in0=xbtok[:, 64 * h:64 * h + 64],
                        scalar1=dtv[:, 4 * s + h:4 * s + h + 1], scalar2=None, op0=ALU.mult)),
                         reads=["xbtok", "dtv"], writes=["xdt"])
                    P.op("dve", (lambda e, h=h: e.tensor_scalar(
                        out=xdtw[:, 64 * h:64 * h + 64], in0=xbtok[:, 64 * h:64 * h + 64],
                        scalar1=dtw[:, h:h + 1], scalar2=None, op0=ALU.mult)),
                         reads=["xbtok", "dtw"], writes=["xdtw"])
                yield
                yield
                yield
                for i in range(2):
                    bank, bkey = gen_bank()
                    for hh in range(2):
                        h = 2 * i + hh
                        P.op("pe", (lambda e, bank=bank, hh=hh, h=h: e.matmul(
                            out=bank[64 * hh:64 * hh + 64, 0:128], lhsT=xdt[:, 64 * h:64 * h + 64],
                            rhs=MT[:, h, :], start=True, stop=False, tile_position=(0, 64 * hh))),
                             reads=["xdt", "MT"], writes=[bkey])
                        P.op("pe", (lambda e, bank=bank, hh=hh, h=h: e.matmul(
                            out=bank[64 * hh:64 * hh + 64, 0:128], lhsT=Hbf[:, 64 * h:64 * h + 64],
                            rhs=CeT[:, h, :], start=False, stop=True, tile_position=(0, 64 * hh))),
                             reads=["Hbf", "CeT"], writes=[bkey])
                    P.op("dve", (lambda e, bank=bank, i=i, o=o: e.scalar_tensor_tensor(
                        out=ytmp[:], in0=xbc[:, i, o:o + 128], scalar=dskip[:, i:i + 1], in1=bank[:, 0:128],
                        op0=ALU.mult, op1=ALU.add)),
                         reads=[bkey, "xbc%d" % i, "dskip"], writes=["ytmp"])
                    P.op("pool", (lambda e, i=i, o=o: e.tensor_tensor(
                        out=yS[:, i, o:o + 128], in0=ytmp[:], in1=zsT[:, i, o:o + 128], op=ALU.mult)),
                         reads=["ytmp", "zsT%d" % i], writes=["yS"])
                yield
                bank, bkey = gen_bank()
                P.op("pe", (lambda e, bank=bank: e.matmul(out=bank[:, 0:256], lhsT=xbtok[:, 256:384],
                                                          rhs=xdtw[:], start=True, stop=True)),
                     reads=["xbtok", "xdtw"], writes=[bkey])
                for h in range(4):
                    P.op("dve", (lambda e, bank=bank, h=h: e.scalar_tensor_tensor(
                        out=Hs[:, 64 * h:64 * h + 64], in0=Hs[:, 64 * h:64 * h + 64],
                        scalar=Erow[:, h, 127:128], in1=bank[:, 64 * h:64 * h + 64],
                        op0=ALU.mult, op1=ALU.add)),
                         reads=[bkey, "Erow", "Hs"], writes=["Hs"])
                P.op("dve", lambda e: e.tensor_copy(out=Hbf[:], in_=Hs[:]), reads=["Hs"], writes=["Hbf"])
                yield
            for i in range(2 if stop >= 1.6 else 0):
                P.op("sp", (lambda e, i=i, t0=t0: e.dma_start(
                    out=ybufs[t0 // AGW].ap()[128 * i:128 * i + 128, t0 % AGW:t0 % AGW + 512], in_=yS[:, i, :])),
                     reads=["yS"], writes=["ybufS%d" % (t0 // AGW)], dma_sem="youtS")

            if T + 1 < (NT if stop > 1 else 0):
                load_h(T + 1)
            yield

        def attn_gen(T):
            t0 = T * 512
            QT = QTs[T % 2]
            zaT = zaTs[T % 2]
            par = T % 2
            steps = []
            for hd in range(2 if stop >= 1.8 else 0):
                for qq in range(2):
                    qi = 2 * T + qq
                    qo = qq * 256
                    nkt = 2 * qi + 2
                    for kt in range(0, nkt, 2):
                        steps.append((hd, qq, qi, qo, nkt, kt))

            def attn_front(idx, hd, qq, qi, qo, nkt, kt0):
                sb_i = idx % 2
                Sv = Sbig[:, sb_i * 1024:(sb_i + 1) * 1024].rearrange("p (t j q) -> p t j q", t=2, j=2)
                skey = "psS%d" % sb_i
                Pt = PT[idx % 2]
                pkey = "PT%d" % (idx % 2)
                diag = (kt0 == nkt - 2)
                for t in range(2):
                    kt = kt0 + t
                    c_lo = 128 if (diag and t == 1) else 0
                    kkey = "KT%d_%d" % (hd, kt // 4)
                    for j in range(2):
                        P.op("pe", (lambda e, j=j, t=t, kt=kt, c_lo=c_lo: e.matmul(
                            out=Sv[:, t, j, c_lo:256], lhsT=KT[:, hd, kt * 128:(kt + 1) * 128],
                            rhs=QT[:, hd, j, qo + c_lo:qo + 256], start=True, stop=(not diag))),
                             reads=[kkey, "QT%d_%d" % (par, hd)], writes=[skey])
                        if diag:
                            P.op("pe", (lambda e, j=j, t=t, c_lo=c_lo: e.matmul(
                                out=Sv[:, t, j, c_lo:c_lo + 128], lhsT=ident_b, rhs=negm_b,
                                start=False, stop=True)),
                                 reads=["cstb"], writes=[skey])
                if not diag:
                    P.op("act", (lambda e: e.activation(
                        out=Pt[:].rearrange("p t j q -> p (t j q)"),
                        in_=Sbig[:, sb_i * 1024:(sb_i + 1) * 1024], func=AF.Exp, scale=0.125)),
                         reads=[skey], writes=[pkey])
                else:
                    P.op("act", (lambda e: e.activation(
                        out=Pt[:, 0].rearrange("p j q -> p (j q)"),
                        in_=Sbig[:, sb_i * 1024:sb_i * 1024 + 512], func=AF.Exp, scale=0.125)),
                         reads=[skey], writes=[pkey])
                    P.op("act", (lambda e: e.activation(
                        out=Pt[:, 1, :, 128:256], in_=Sv[:, 1, :, 128:256], func=AF.Exp, scale=0.125)),
                         reads=[skey], writes=[pkey])

            def attn_back(idx, hd, qq, qi, qo, nkt, kt0):
                Pt = PT[idx % 2]
                pkey = "PT%d" % (idx % 2)
                Ob = banks[4].rearrange("p (j q) -> p j q", j=2)
                Lb = banks[5].rearrange("p (j q) -> p j q", j=2)
                diag = (kt0 == nkt - 2)
                for t in range(2):
                    kt = kt0 + t
                    c_lo = 128 if (diag and t == 1) else 0
                    if c_lo == 0:
                        Pt2 = Pt[:, t].rearrange("p j q -> p (j q)")
                        P.op("pe", (lambda e, kt=kt, Pt2=Pt2: e.matmul(
                            out=banks[4][:, :], lhsT=Vt[:, kt, 128 * hd:128 * hd + 128], rhs=Pt2,
                            start=(kt == 0), stop=False)),
                             reads=[pkey, "V%d" % kt], writes=["psO"])
                        P.op("pe", (lambda e, kt=kt, Pt2=Pt2: e.matmul(
                            out=banks[5][:, :], lhsT=ones_b, rhs=Pt2,
                            start=(kt == 0), stop=False)),
                             reads=[pkey, "cstb"], writes=["psL"])
                    else:
                        for j in range(2):
                            P.op("pe", (lambda e, j=j, kt=kt, t=t: e.matmul(
                                out=Ob[:, j, 128:256], lhsT=Vt[:, kt, 128 * hd:128 * hd + 128],
                                rhs=Pt[:, t, j, 128:256], start=False, stop=(j == 1))),
                                 reads=[pkey, "V%d" % kt], writes=["psO"])
                            P.op("pe", (lambda e, j=j, kt=kt, t=t: e.matmul(
                                out=Lb[:, j, 128:256], lhsT=ones_b, rhs=Pt[:, t, j, 128:256],
                                start=False, stop=(j == 1))),
                                 reads=[pkey, "cstb"], writes=["psL"])
                kt = kt0 + 1
                if kt != nkt - 1:
                    return None
                P.op("act", lambda e: e.activation(out=rl[:].rearrange("p j q -> p (j q)"), in_=banks[5][:, :],
                                                   func=AF.Ln), reads=["psL"], writes=["rl"])
                P.op("act", lambda e: e.activation(out=rl[:].rearrange("p j q -> p (j q)"),
                                                   in_=rl[:].rearrange("p j q -> p (j q)"), func=AF.Exp, scale=-1.0),
                     reads=["rl"], writes=["rl"])
                P.op("dve", lambda e: e.tensor_tensor(out=On[:].rearrange("p j q -> p (j q)"), in0=banks[4][:, :],
                                                      in1=rl[:].rearrange("p j q -> p (j q)"), op=ALU.mult),
                     reads=["psO", "rl"], writes=["On"])
                P.op("dve", lambda e: e.scalar_tensor_tensor(out=Od[:], in0=On[:, 1, :], scalar=neglam[:, 0:1],
                                                             in1=On[:, 0, :], op0=ALU.mult, op1=ALU.add),
                     reads=["On", "neglam"], writes=["Od"])
                P.op("dve", lambda e: e.tensor_tensor(out=sqb[:], in0=Od[:], in1=Od[:], op=ALU.mult),
                     reads=["Od"], writes=["sqb"])
                return lambda: attn_fin(hd, qo)

            def attn_fin(hd, qo):
                P.op("pe", (lambda e: e.matmul(out=banks[5][:, 0:256], lhsT=ones_b, rhs=sqb[:],
                                               start=True, stop=True)),
                     reads=["sqb", "cstb"], writes=["psL"])
                P.op("dve", (lambda e: e.tensor_scalar(
                    out=rstd[:], in0=banks[5][:, 0:256], scalar1=1.0 / 128.0, scalar2=EPS,
                    op0=ALU.mult, op1=ALU.add)), reads=["psL"], writes=["rstd"])
                P.op("act", lambda e: e.activation(out=rstd[:], in_=rstd[:], func=AF.Ln),
                     reads=["rstd"], writes=["rstd"])
                P.op("act", lambda e: e.activation(out=rstd[:], in_=rstd[:], func=AF.Exp, scale=-0.5),
                     reads=["rstd"], writes=["rstd"])
                P.op("dve", lambda e: e.scalar_tensor_tensor(out=rstd[:], in0=Od[:], scalar=anw[:, 0:1],
                                                             in1=rstd[:], op0=ALU.mult, op1=ALU.mult),
                     reads=["Od", "anw", "rstd"], writes=["rstd"])
                P.op("pool", (lambda e: e.tensor_tensor(
                    out=yA[:, hd, qo:qo + 256], in0=rstd[:], in1=zaT[:, hd, qo:qo + 256], op=ALU.mult)),
                     reads=["rstd", "zaT%d_%d" % (par, hd)], writes=["yA"])

            for idx, st_ in enumerate(steps):
                attn_front(idx, *st_)
                fin = attn_back(idx - 1, *steps[idx - 1]) if idx >= 1 else None
                yield (fin is not None)
                if fin is not None:
                    fin()
            if steps:
                fin = attn_back(len(steps) - 1, *steps[-1])
                if fin is not None:
                    fin()
            for hd in range(2 if stop >= 1.8 else 0):
                P.op("sp", (lambda e, hd=hd, t0=t0: e.dma_start(
                    out=ybufs[t0 // AGW].ap()[256 + 128 * hd:256 + 128 * hd + 128, t0 % AGW:t0 % AGW + 512],
                    in_=yA[:, hd, :])),
                     reads=["yA"], writes=["ybufA%d" % (t0 // AGW)], dma_sem="youtA")
            if stop >= 3 and (t0 + 512) % AGW == 0:
                a_i = t0 // AGW
                P.op("pool", (lambda e, a_i=a_i: e.collective_compute(
                    "AllGather", ALU.bypass, replica_groups=[[0, 1, 2, 3], [4, 5, 6, 7]],
                    ins=[ybufs[a_i].ap().opt()], outs=[ygaths[a_i].ap().opt()])),
                     reads=["ybufS%d" % a_i, "ybufA%d" % a_i], writes=["ygath%d" % a_i], dma_sem="cc%d" % a_i, inc=1)

            yield

        def drain(g):
            for _ in g:
                pass

        OTHER_YIELDS = 44
        NTr = NT if stop > 1 else 0
        if NTr:
            load_h(0)
            OTHER_YIELDS = sum(1 for _ in other_gen(0))
        for T in range(NTr):
            ag = attn_gen(T)
            n_attn = 8 * T + 6
            if T + 1 < NTr:
                og = other_gen(T + 1)
                n_other = OTHER_YIELDS
                done_o = 0
                alive = True
                extra = 0
                for k_, bnd in enumerate(ag):
                    if bnd:
                        extra += 2
                    want = min(n_other, ((k_ + 1) * n_other * 21) // (n_attn * 20) + 1 + extra)
                    while alive and done_o < want:
                        try:
                            next(og)
                            done_o += 1
                        except StopIteration:
                            alive = False
                if alive:
                    drain(og)
            else:
                drain(ag)

        if debug and stop >= 3:
            for a_i in range(NAG):
                P.op("sp", (lambda e, a_i=a_i: e.dma_start(out=dbg_d[:, a_i * AGW:(a_i + 1) * AGW],
                                                          in_=ygaths[a_i].ap())),
                     reads=["ygath%d" % a_i], writes=["dbg"], dma_sem="dbg")

        P.barrier()
        wout_sb = KT[:].rearrange("p h l -> p (h l)")
        big_ok = (2 * L >= 16 * 1024)
        if not big_ok:
            wout_t = sb("wout_t", [128, 16 * 1024], BF16)
            wout_sb = wout_t[:]
        wout_v = wout_sb[:, 0:16 * 1024].rearrange("p (c n) -> p c n", c=16)
        if big_ok:
            vflat = Vt[:].rearrange("p a b -> p (a b)")
            Yts = [vflat[:, (2 * i) * 16 * ST3:(2 * i + 1) * 16 * ST3].rearrange("p (c t) -> p c t", c=16)
                   for i in range(2)]
            Yns = [vflat[:, (2 * i + 1) * 16 * ST3:(2 * i + 2) * 16 * ST3].rearrange("p (c t) -> p c t", c=16)
                   for i in range(2)]
        else:
            Yts = [sb("Yt%d" % i, [128, 16, ST3], BF16)[:] for i in range(1)] * 2
            Yns = [sb("Yn%d" % i, [128, 16, ST3], BF16)[:] for i in range(1)] * 2
        nbuf3 = 2 if big_ok else 1
        sq3s = [xbc[:, i, 0:ST3] for i in range(4)]
        rs3 = etmp[:, 0:2 * ST3].rearrange("p (g t) -> p g t", g=2)
        x0f = xTs[0][:].rearrange("p a b -> p (a b)")
        x1f = xTs[1][:].rearrange("p a b -> p (a b)")
        lng = x0f[:, 0:1024]
        lnb = x0f[:, 1024:2048]
        pres = [x0f[:, 2048:3072], hT[:].rearrange("p a b -> p (a b)").bitcast(F32)[:, 0:1024]]
        junk = x0f[:, 3072:4096]
        xt = [x1f[:, 0:1024], x1f[:, 1024:2048]]
        ot = [x1f[:, 2048:3072], x1f[:, 3072:4096]]

        for c in range(16):
            P.op("pool", (lambda e, c=c: e.dma_start(out=wout_v[:, c, :], in_=wout_d[c])),
                 writes=["wout"], dma_sem="wout")
        P.op("sp", lambda e: e.dma_start(out=lng, in_=lng_d), writes=["lnp"], dma_sem="misc")
        P.op("sp", lambda e: e.dma_start(out=lnb, in_=lnb_d), writes=["lnp"], dma_sem="misc")

        dyn = {}
        NS3r = NS3 if stop >= 4 else 0

        def load_yt(S):
            b3 = S % nbuf3
            ygv = ygaths[S].ap().rearrange("(c p) t -> p c t", p=128)
            for c8 in range(2):
                P.op("sp", (lambda e, c8=c8, ygv=ygv, b3=b3: e.dma_start(
                    out=Yts[b3][:, 8 * c8:8 * c8 + 8, :],
                    in_=ygv[:, 8 * c8:8 * c8 + 8, bass.ds(dyn["roff"], ST3)])),
                     reads=["ygath%d" % S], writes=["Yt%d" % b3], dma_sem="yt%d" % b3)

        def norm_yt(S):
            b3 = S % nbuf3
            Yt, Yn = Yts[b3], Yns[b3]
            for g in range(2):
                bank, bkey = gen_bank()
                chunks = [8 * g + 4 * rr + m for rr in range(2) for m in range(2)]
                for n_i, c in enumerate(chunks):
                    P.op("dve", (lambda e, c=c, n_i=n_i, Yt=Yt: e.tensor_tensor(
                        out=sq3s[n_i], in0=Yt[:, c, :], in1=Yt[:, c, :], op=ALU.mult)),
                         reads=["Yt%d" % b3], writes=["sq3_%d" % n_i])
                for n_i, c in enumerate(chunks):
                    P.op("pe", (lambda e, bank=bank, n_i=n_i: e.matmul(
                        out=bank[:, 0:ST3], lhsT=ones_b, rhs=sq3s[n_i], start=(n_i == 0), stop=(n_i == 3))),
                         reads=["sq3_%d" % n_i, "cstb"], writes=[bkey])
                P.op("dve", (lambda e, bank=bank, g=g: e.tensor_scalar(
                    out=rs3[:, g, :], in0=bank[:, 0:ST3], scalar1=1.0 / 512.0, scalar2=EPS,
                    op0=ALU.mult, op1=ALU.add)), reads=[bkey], writes=["rs3"])
                P.op("act", (lambda e, g=g: e.activation(out=rs3[:, g, :], in_=rs3[:, g, :], func=AF.Ln)),
                     reads=["rs3"], writes=["rs3"])
                P.op("act", (lambda e, g=g: e.activation(out=rs3[:, g, :], in_=rs3[:, g, :], func=AF.Exp,
                                                         scale=-0.5)), reads=["rs3"], writes=["rs3"])
                for c in chunks:
                    P.op("dve", (lambda e, c=c, g=g, Yt=Yt, Yn=Yn: e.scalar_tensor_tensor(
                        out=Yn[:, c, :], in0=Yt[:, c, :], scalar=nw[:, c:c + 1], in1=rs3[:, g, :],
                        op0=ALU.mult, op1=ALU.mult)), reads=["Yt%d" % b3, "nw", "rs3"], writes=["Yn%d" % b3])

        def proj_ln(S, st):
            b3 = S % nbuf3
            Yt, Yn = Yts[b3], Yns[b3]
            gi = S * (ST3 // 128) + st
            xs = gi % 2
            pre = pres[xs]
            pk = "pre%d" % xs
            sm = small[:, 8 * xs:8 * xs + 8]
            sk = "small%d" % xs
            P.op("sp", (lambda e: e.dma_start(out=xt[xs], in_=xtok_d[gi * 128:(gi + 1) * 128, :])),
                 writes=["xt%d" % xs], dma_sem="xt%d" % xs)
            for half in range(2):
                bank, bkey = gen_bank()
                for c in range(16):
                    src = Yn if c % 4 < 2 else Yt
                    skey = ("Yn%d" if c % 4 < 2 else "Yt%d") % b3
                    P.op("pe", (lambda e, bank=bank, c=c, half=half, src=src: e.matmul(
                        out=bank[:, :], lhsT=src[:, c, st * 128:(st + 1) * 128],
                        rhs=wout_v[:, c, half * 512:(half + 1) * 512], start=(c == 0), stop=(c == 15))),
                         reads=[skey, "wout"], writes=[bkey])
                P.op("dve", (lambda e, bank=bank, half=half: e.tensor_tensor(
                    out=pre[:, half * 512:(half + 1) * 512], in0=bank[:, :],
                    in1=gate_bc[:, half * 512:(half + 1) * 512], op=ALU.mult)),
                     reads=[bkey, "gate_bc"], writes=[pk])
            P.op("dve", (lambda e: e.scalar_tensor_tensor(out=pre, in0=xt[xs], scalar=ALPHA, in1=pre,
                                                          op0=ALU.mult, op1=ALU.add)),
                 reads=[pk, "xt%d" % xs], writes=[pk])
            P.op("pool", lambda e: e.memset(sm[:, 0:2], 0.0), writes=[sk])
            P.op("act", lambda e: e.activation(out=junk, in_=pre, func=AF.Identity, accum_out=sm[:, 0:1]),
                 reads=[pk, sk], writes=["junk", sk])
            P.op("act", lambda e: e.activation(out=junk, in_=pre, func=AF.Square, accum_out=sm[:, 1:2]),
                 reads=[pk, sk], writes=["junk", sk])
            P.op("dve", lambda e: e.tensor_scalar(out=sm[:, 2:4], in0=sm[:, 0:2], scalar1=1.0 / 1024.0,
                                                  scalar2=None, op0=ALU.mult), reads=[sk], writes=[sk])
            P.op("dve", lambda e: e.tensor_tensor(out=sm[:, 4:5], in0=sm[:, 2:3], in1=sm[:, 2:3],
                                                  op=ALU.mult), reads=[sk], writes=[sk])
            P.op("dve", lambda e: e.tensor_tensor(out=sm[:, 5:6], in0=sm[:, 3:4], in1=sm[:, 4:5],
                                                  op=ALU.subtract), reads=[sk], writes=[sk])
            P.op("dve", lambda e: e.tensor_scalar(out=sm[:, 6:7], in0=sm[:, 5:6], scalar1=EPS, scalar2=None,
                                                  op0=ALU.add), reads=[sk], writes=[sk])
            P.op("act", lambda e: e.activation(out=sm[:, 6:7], in_=sm[:, 6:7], func=AF.Ln),
                 reads=[sk], writes=[sk])
            P.op("act", lambda e: e.activation(out=sm[:, 6:7], in_=sm[:, 6:7], func=AF.Exp, scale=-0.5),
                 reads=[sk], writes=[sk])
            P.op("dve", lambda e: e.scalar_tensor_tensor(out=sm[:, 7:8], in0=sm[:, 2:3], scalar=-1.0,
                                                         in1=sm[:, 6:7], op0=ALU.mult, op1=ALU.mult),
                 reads=[sk], writes=[sk])
            P.op("act", lambda e: e.activation(out=pre, in_=pre, func=AF.Identity, bias=sm[:, 7:8],
                                               scale=sm[:, 6:7]), reads=[pk, sk], writes=[pk])
            P.op("dve", lambda e: e.tensor_tensor(out=pre, in0=pre, in1=lng, op=ALU.mult),
                 reads=[pk, "lnp"], writes=[pk])
            P.op("pool", (lambda e: e.tensor_tensor(out=ot[xs], in0=pre, in1=lnb, op=ALU.add)),
                 reads=[pk, "lnp"], writes=["ot%d" % xs])
            P.op("sp", (lambda e: e.dma_start(out=out_d[gi * 128:(gi + 1) * 128, :], in_=ot[xs])),
                 reads=["ot%d" % xs], writes=["out%d" % gi], dma_sem="o%d" % xs)

        if NS3r:
            load_yt(0)
            norm_yt(0)
        for S in range(NS3r):
            if S + 1 < NS3r:
                load_yt(S + 1)
            for st in range(ST3 // 128):
                proj_ln(S, st)
                if st == 0 and S + 1 < NS3r:
                    norm_yt(S + 1)
        P.wait_all("sp")

        with nc.Block() as block:
            @block.tensor
            def _(e):
                P.replay("pe", e, sems, dma_sems)

            @block.scalar
            def _(e):
                P.replay("act", e, sems, dma_sems)

            @block.vector
            def _(e):
                P.replay("dve", e, sems, dma_sems)

            @block.gpsimd
            def _(e):
                P.replay("pool", e, sems, dma_sems)

            @block.sync
            def _(e):
                rank = e.partition_id() % 4
                dyn["roff"] = e.snap(rank * ST3, min_val=0, max_val=3 * ST3)
                P.replay("sp", e, sems, dma_sems)
    return nc


def make_consts():
    s = np.arange(128)[:, None]
    l = np.arange(128)[None, :]
    tri_incl = (s <= l).astype(np.float32)
    tri_su = (s > l).astype(np.float32)
    negm = np.where(s <= l, 0.0, NEG).astype(np.float32)
    ones = np.ones((128, 128), np.float32)
    ident = np.eye(128, dtype=np.float32)
    return np.ascontiguousarray(np.concatenate([tri_incl, tri_su, negm, ones, ident], axis=1))


def tok_idx(L, r):
    nag = max(L // 1024, 1)
    agw = L // nag
    st3 = agw // 4
    return np.concatenate([a * agw + r * st3 + np.arange(st3) for a in range(nag)])


def prep_inputs(inp, L):
    f = lambda a: np.ascontiguousarray(np.asarray(a, dtype=np.float32))
    x = f(inp["x"]); c = f(inp["c"])
    w_ada = f(inp["w_ada"])[0]; b_ada = f(inp["b_ada"])[0]; w_in = f(inp["w_in"])[0]
    conv_w = f(inp["conv_w"])[0]; conv_b = f(inp["conv_b"])[0]
    dt_bias = f(inp["dt_bias"])[0]; a_log = f(inp["a_log"])[0]; d_skip = f(inp["d_skip"])[0]
    ssd_norm_w = f(inp["ssd_norm_w"])[0]
    lq1 = f(inp["lambda_q1"])[0]; lk1 = f(inp["lambda_k1"])[0]
    lq2 = f(inp["lambda_q2"])[0]; lk2 = f(inp["lambda_k2"])[0]
    attn_norm_w = f(inp["attn_norm_w"])[0]; w_out = f(inp["w_out"])[0]
    ln_g = f(inp["ln_g"])[0]; ln_b = f(inp["ln_b"])[0]
    LQ = L // 4
    cst = make_consts()
    wada_r = np.ascontiguousarray(w_ada.reshape(8, 128, 3072))
    bada_col = np.ascontiguousarray(b_ada.reshape(24, 128).T)
    bgate = np.ascontiguousarray(np.tile(b_ada[None, 2048:3072], (128, 1)))
    lamv = np.ascontiguousarray(np.tile(np.concatenate([lq1, lk1, lq2, lk2])[None, :], (128, 1)))
    lng = np.ascontiguousarray(np.tile(ln_g[None, :], (128, 1)))
    lnb = np.ascontiguousarray(np.tile(ln_b[None, :], (128, 1)))
    anw = np.ascontiguousarray(attn_norm_w.reshape(128, 1))
    rows = []
    for r in range(4):
        rows += list(range(256 * r, 256 * r + 256)) + list(range(1024 + 256 * r, 1024 + 256 * r + 256))
    rows = np.array(rows)
    wout_r = np.ascontiguousarray(w_out[rows].reshape(16, 128, 1024))
    nw = np.ones((128, 16), np.float32)
    for r in range(4):
        for m in range(2):
            nw[:, 4 * r + m] = ssd_norm_w[256 * r + 128 * m:256 * r + 128 * m + 128]
    maps = []
    for core in range(8):
        b, hp = core // 4, core % 4
        g = hp // 2
        cols = np.concatenate([
            np.arange(256 * hp, 256 * hp + 256),
            OFF_XBC + np.arange(256 * hp, 256 * hp + 256),
            OFF_XBC + 1024 + 128 * g + np.arange(128),
            OFF_XBC + 1280 + 128 * g + np.arange(128),
            OFF_Q + 256 * hp + np.arange(256),
            OFF_K + 256 * hp + np.arange(256),
            OFF_ZA + 256 * hp + np.arange(256),
            OFF_V + 256 * hp + np.arange(256),
            OFF_DT + 4 * hp + np.arange(4),
        ])
        win = np.zeros((1024, NCOLS), np.float32)
        win[:, :cols.size] = w_in[:, cols]
        chans = [256 * hp + np.arange(128), 256 * hp + 128 + np.arange(128),
                 1024 + 128 * g + np.arange(128), 1280 + 128 * g + np.arange(128)]
        convw = np.zeros((128, 16), np.float32)
        convb = np.zeros((128, 4), np.float32)
        for ch in range(4):
            convw[:, 4 * ch:4 * ch + 4] = conv_w[:, chans[ch]].T
            convb[:, ch] = conv_b[chans[ch]]
        hsel = np.arange(4 * hp, 4 * hp + 4)
        dtb = np.tile(np.tile(dt_bias[hsel], 4)[None, :], (128, 1))
        alog = np.tile(np.tile(a_log[hsel], 4)[None, :], (128, 1))
        dskip = np.zeros((128, 2), np.float32)
        for i in range(2):
            dskip[:, i] = np.repeat(d_skip[hsel[2 * i:2 * i + 2]], 64)
        maps.append({
            "xT": np.ascontiguousarray(x[b, :L].T.reshape(8, 128, L)),
            "xtok": np.ascontiguousarray(x[b, tok_idx(L, hp), :]),
            "ccol": np.ascontiguousarray(c[b].reshape(8, 128).T),
            "wada": wada_r, "bada_col": bada_col, "bgate": bgate,
            "win": np.ascontiguousarray(win.reshape(8, 128, NCOLS)),
            "convw": convw, "convb": convb,
            "dtb": np.ascontiguousarray(dtb.astype(np.float32)),
            "alog": np.ascontiguousarray(alog.astype(np.float32)),
            "dskip": dskip, "lamv": lamv, "anw": anw, "nw": nw, "wout": wout_r,
            "lng": lng, "lnb": lnb, "cst": cst,
        })
    return maps


def run(inp, L, debug=False, stop=9):
    nc = build_program(L, debug=debug, stop=stop)
    maps = prep_inputs(inp, L)
    res = run_bass_kernel_spmd(nc, maps, core_ids=list(range(8)))
    LQ = L // 4
    out = np.zeros((2, L, 1024), np.float32)
    for core in range(8):
        b, hp = core // 4, core % 4
        out[b, tok_idx(L, hp), :] = np.asarray(res.results[core]["out"])
    if debug:
        return out, [np.asarray(res.results[core]["dbg"]) for core in range(8)]
    return out


def kernel(**inputs):
    return run(inputs, SEQ)
```
